# Optimizing a Trainium2 kernel written in Bass

```python
import math
import jax, jax.numpy as jnp
from jax import lax
import numpy as np

D_MODEL = 1024
BATCH = 8
SEQ = 4096
DEPTH = 4

HEAD_DIM = 64
N_HEADS = D_MODEL // HEAD_DIM
SWA_KV_HEADS = max(1, N_HEADS // 8)
N_MIXERS = 3
SB_BLOCK = 128
MOBA_BLOCK = 256
MOBA_TOPK = 3
MOBA_QCHUNK = 16
SWA_WINDOW = 128
ROPE_THETA = 10000.0
D_FF = -(-8 * D_MODEL // (3 * 256)) * 256
EPS = 1e-6

kernel_name = "hybrid_sb_moba_swa_adaln_trunk"


def rms_norm(x, gain):
    xf = x.astype(jnp.float32)
    var = jnp.mean(xf * xf, axis=-1, keepdims=True)
    return (xf * lax.rsqrt(var + EPS)).astype(x.dtype) * gain


def modulate(h, shift, scale):
    return h * (1.0 + scale) + shift


def split_heads(t, n):
    b, s, _ = t.shape
    return t.reshape(b, s, n, HEAD_DIM).transpose(0, 2, 1, 3)


def merge_heads(t):
    b, n, s, d = t.shape
    return t.transpose(0, 2, 1, 3).reshape(b, s, n * d)


def rope_tables(seq_len):
    inv_freq = 1.0 / (ROPE_THETA ** (jnp.arange(0, HEAD_DIM, 2, dtype=jnp.float32) / HEAD_DIM))
    ang = jnp.arange(seq_len, dtype=jnp.float32)[:, None] * inv_freq[None, :]
    return jnp.cos(ang), jnp.sin(ang)


def apply_rope(x, cos, sin):
    cos = cos.astype(x.dtype)
    sin = sin.astype(x.dtype)
    x1, x2 = jnp.split(x, 2, axis=-1)
    return jnp.concatenate([x1 * cos - x2 * sin, x2 * cos + x1 * sin], axis=-1)


def stick_breaking_attention(h, w_in, w_out):
    b, s_len, _ = h.shape
    q, k, v = jnp.split(h @ w_in, 3, axis=-1)
    q, k, v = split_heads(q, N_HEADS), split_heads(k, N_HEADS), split_heads(v, N_HEADS)
    nblk = s_len // SB_BLOCK
    q_blocks = q.reshape(b, N_HEADS, nblk, SB_BLOCK, HEAD_DIM).transpose(2, 0, 1, 3, 4)
    k_pos = jnp.arange(s_len)
    scale = HEAD_DIM ** -0.5

    def block(args):
        qb, bi = args
        q_pos = bi * SB_BLOCK + jnp.arange(SB_BLOCK)
        z = jnp.einsum('bhqd,bhkd->bhqk', qb, k).astype(jnp.float32) * scale
        past = k_pos[None, :] < q_pos[:, None]
        log_beta = jax.nn.log_sigmoid(z)
        log_keep = jnp.where(past, jax.nn.log_sigmoid(-z), 0.0)
        later = lax.cumsum(log_keep, axis=3, reverse=True) - log_keep
        a = jnp.where(past, jnp.exp(log_beta + later), 0.0)
        return jnp.einsum('bhqk,bhkd->bhqd', a.astype(v.dtype), v)

    o = lax.map(block, (q_blocks, jnp.arange(nblk)))
    o = o.transpose(1, 2, 0, 3, 4).reshape(b, N_HEADS, s_len, HEAD_DIM)
    return merge_heads(o) @ w_out


def moba_attention(h, w_in, qk_gain, w_out, cos, sin):
    b, s_len, _ = h.shape
    q, k, v = jnp.split(h @ w_in, 3, axis=-1)
    q, k, v = split_heads(q, N_HEADS), split_heads(k, N_HEADS), split_heads(v, N_HEADS)
    q = apply_rope(rms_norm(q, qk_gain[0]), cos, sin)
    k = apply_rope(rms_norm(k, qk_gain[1]), cos, sin)
    pad = (-s_len) % MOBA_BLOCK
    padw = ((0, 0), (0, 0), (0, pad), (0, 0))
    q, k, v = jnp.pad(q, padw), jnp.pad(k, padw), jnp.pad(v, padw)
    s_pad = s_len + pad
    nb = s_pad // MOBA_BLOCK
    topk = min(MOBA_TOPK, nb)
    k_blk = k.reshape(b, N_HEADS, nb, MOBA_BLOCK, HEAD_DIM)
    v_blk = v.reshape(b, N_HEADS, nb, MOBA_BLOCK, HEAD_DIM)
    k_mean = jnp.mean(k_blk.astype(jnp.float32), axis=3).astype(k.dtype)
    n_chunks = s_pad // MOBA_QCHUNK
    q_chunks = q.reshape(b, N_HEADS, n_chunks, MOBA_QCHUNK, HEAD_DIM).transpose(2, 0, 1, 3, 4)
    b_idx = jnp.arange(b)[:, None, None, None]
    h_idx = jnp.arange(N_HEADS)[None, :, None, None]
    blk_ids = jnp.arange(nb)
    in_blk = jnp.arange(MOBA_BLOCK)
    scale = HEAD_DIM ** -0.5

    def chunk(args):
        qc, ci = args
        q_pos = ci * MOBA_QCHUNK + jnp.arange(MOBA_QCHUNK)
        own = (ci * MOBA_QCHUNK) // MOBA_BLOCK
        gate = jnp.einsum('bhqd,bhnd->bhqn', qc, k_mean).astype(jnp.float32)
        gate = jnp.where(blk_ids < own, gate, -jnp.inf)
        top_val, top_idx = lax.top_k(gate, topk)
        sel_ok = jnp.isfinite(top_val)
        kg = k_blk[b_idx, h_idx, top_idx]
        vg = v_blk[b_idx, h_idx, top_idx]
        s_sel = jnp.einsum('bhqd,bhqnkd->bhqnk', qc, kg).astype(jnp.float32) * scale
        s_sel = jnp.where(sel_ok[..., None], s_sel, -jnp.inf)
        k_own = lax.dynamic_slice_in_dim(k, own * MOBA_BLOCK, MOBA_BLOCK, axis=2)
        v_own = lax.dynamic_slice_in_dim(v, own * MOBA_BLOCK, MOBA_BLOCK, axis=2)
        s_own = jnp.einsum('bhqd,bhkd->bhqk', qc, k_own).astype(jnp.float32) * scale
        own_pos = own * MOBA_BLOCK + in_blk
        s_own = jnp.where(own_pos[None, :] <= q_pos[:, None], s_own, -jnp.inf)
        qn = qc.shape[2]
        logits = jnp.concatenate([s_sel.reshape(b, N_HEADS, qn, topk * MOBA_BLOCK), s_own], axis=-1)
        p = jax.nn.softmax(logits, axis=-1).astype(v.dtype)
        p_sel = p[..., :topk * MOBA_BLOCK].reshape(b, N_HEADS, qn, topk, MOBA_BLOCK)
        p_own = p[..., topk * MOBA_BLOCK:]
        return (jnp.einsum('bhqnk,bhqnkd->bhqd', p_sel, vg)
                + jnp.einsum('bhqk,bhkd->bhqd', p_own, v_own))

    o = lax.map(chunk, (q_chunks, jnp.arange(n_chunks)))
    o = o.transpose(1, 2, 0, 3, 4).reshape(b, N_HEADS, s_pad, HEAD_DIM)[:, :, :s_len]
    return merge_heads(o) @ w_out


def swa_sink_attention(h, w_in, qk_gain, sinks, w_out, cos, sin):
    b, s_len, _ = h.shape
    qd, kvd = N_HEADS * HEAD_DIM, SWA_KV_HEADS * HEAD_DIM
    q, k, v = jnp.split(h @ w_in, [qd, qd + kvd], axis=-1)
    q, k, v = split_heads(q, N_HEADS), split_heads(k, SWA_KV_HEADS), split_heads(v, SWA_KV_HEADS)
    q = apply_rope(rms_norm(q, qk_gain[0]), cos, sin)
    k = apply_rope(rms_norm(k, qk_gain[1]), cos, sin)
    w = SWA_WINDOW
    nb = s_len // w
    g = N_HEADS // SWA_KV_HEADS
    qb = q.reshape(b, SWA_KV_HEADS, g, nb, w, HEAD_DIM)

    def band(t):
        tb = t.reshape(b, SWA_KV_HEADS, nb, w, HEAD_DIM)
        prev = jnp.pad(tb, ((0, 0), (0, 0), (1, 0), (0, 0), (0, 0)))[:, :, :-1]
        return jnp.concatenate([prev, tb], axis=3)

    kb, vb = band(k), band(v)
    s = jnp.einsum('bkgnqd,bknsd->bkgnqs', qb, kb).astype(jnp.float32) * (HEAD_DIM ** -0.5)
    q_off = jnp.arange(w)[:, None] + w
    k_off = jnp.arange(2 * w)[None, :]
    rel = q_off - k_off
    in_win = (rel >= 0) & (rel < SWA_WINDOW)
    valid = in_win[None] & ((jnp.arange(nb)[:, None, None] > 0) | (k_off >= w)[None])
    s = jnp.where(valid, s, -jnp.inf)
    sink = sinks.astype(jnp.float32).reshape(SWA_KV_HEADS, g)[None, :, :, None, None, None]
    m = jnp.maximum(jnp.max(s, axis=-1, keepdims=True), sink)
    e = jnp.exp(s - m)
    p = e / (jnp.sum(e, axis=-1, keepdims=True) + jnp.exp(sink - m))
    o = jnp.einsum('bkgnqs,bknsd->bkgnqd', p.astype(v.dtype), vb)
    o = o.reshape(b, N_HEADS, s_len, HEAD_DIM)
    return merge_heads(o) @ w_out


def swiglu(h, w_gate, w_up, w_down):
    return (jax.nn.silu(h @ w_gate) * (h @ w_up)) @ w_down


def setup_inputs(seed: int = 0) -> dict:
    key = jax.random.key(seed)
    ks = jax.random.split(key, 20)
    d = D_MODEL
    hd = N_HEADS * HEAD_DIM
    n_sb, n_moba, n_swa = (DEPTH + 2) // 3, (DEPTH + 1) // 3, DEPTH // 3

    def nrm(k, shape, scale):
        return jax.random.normal(k, shape, jnp.float32) * scale

    return {
        "x": nrm(ks[0], (BATCH, SEQ, d), 1.0),
        "c": nrm(ks[1], (BATCH, d), 1.0),
        "ada_w": nrm(ks[2], (DEPTH, d, 6 * d), 0.5 * d ** -0.5),
        "ada_b": nrm(ks[3], (DEPTH, 6 * d), 0.02),
        "norm_gain": 1.0 + nrm(ks[4], (DEPTH, 2, d), 0.02),
        "ffn_w_gate": nrm(ks[5], (DEPTH, d, D_FF), d ** -0.5),
        "ffn_w_up": nrm(ks[6], (DEPTH, d, D_FF), d ** -0.5),
        "ffn_w_down": nrm(ks[7], (DEPTH, D_FF, d), D_FF ** -0.5),
        "sb_w_in": nrm(ks[8], (n_sb, d, 3 * hd), d ** -0.5),
        "sb_w_out": nrm(ks[9], (n_sb, hd, d), hd ** -0.5),
        "moba_w_in": nrm(ks[10], (n_moba, d, 3 * hd), d ** -0.5),
        "moba_qk_gain": 1.0 + nrm(ks[11], (n_moba, 2, HEAD_DIM), 0.02),
        "moba_w_out": nrm(ks[12], (n_moba, hd, d), hd ** -0.5),
        "swa_w_in": nrm(ks[13], (n_swa, d, (N_HEADS + 2 * SWA_KV_HEADS) * HEAD_DIM), d ** -0.5),
        "swa_qk_gain": 1.0 + nrm(ks[14], (n_swa, 2, HEAD_DIM), 0.02),
        "swa_sinks": nrm(ks[15], (n_swa, N_HEADS), 0.5),
        "swa_w_out": nrm(ks[16], (n_swa, hd, d), hd ** -0.5),
    }


def reference(x, c, ada_w, ada_b, norm_gain, ffn_w_gate, ffn_w_up, ffn_w_down,
              sb_w_in, sb_w_out, moba_w_in, moba_qk_gain, moba_w_out,
              swa_w_in, swa_qk_gain, swa_sinks, swa_w_out):
    s_len = x.shape[1]
    cos, sin = rope_tables(s_len)
    c_act = jax.nn.silu(c)
    for i in range(DEPTH):
        mod = c_act @ ada_w[i] + ada_b[i]
        sh1, sc1, g1, sh2, sc2, g2 = [m[:, None, :] for m in jnp.split(mod, 6, axis=-1)]
        h = modulate(rms_norm(x, norm_gain[i, 0]), sh1, sc1)
        kind, j = i % N_MIXERS, i // N_MIXERS
        if kind == 0:
            y = stick_breaking_attention(h, sb_w_in[j], sb_w_out[j])
        elif kind == 1:
            y = moba_attention(h, moba_w_in[j], moba_qk_gain[j], moba_w_out[j], cos, sin)
        else:
            y = swa_sink_attention(h, swa_w_in[j], swa_qk_gain[j], swa_sinks[j], swa_w_out[j], cos, sin)
        x = x + g1 * y
        h = modulate(rms_norm(x, norm_gain[i, 1]), sh2, sc2)
        x = x + g2 * swiglu(h, ffn_w_gate[i], ffn_w_up[i], ffn_w_down[i])
    return x
```

```python
import contextlib
import numpy as np
import ml_dtypes
import concourse.bass as bass
import concourse.mybir as mybir
from concourse.bass_utils import run_bass_kernel_spmd

F32 = mybir.dt.float32
BF16 = mybir.dt.bfloat16
AF = mybir.ActivationFunctionType
ALU = mybir.AluOpType
AX = mybir.AxisListType

S = 4096
D = 1024
DFF = 2816
NT = 8
EPS = 1e-6
NEG = -30000.0
EPOCH = 20000
DMA_SLOTS = {"sp": 16, "pool": 8, "act": 8}


class Buf:
    __slots__ = ("name", "writers", "readers", "excl")

    def __init__(self, name, excl=False):
        self.name = name
        self.excl = excl
        self.writers = {}
        self.readers = {}


class Op:
    __slots__ = ("stream", "fn", "deps", "signal", "val", "key", "is_dma", "prev_slot_op", "idx")


class Prog:
    STREAMS = ("pe", "act", "dve", "pool", "sp")

    def __init__(self, nc):
        self.nc = nc
        self.streams = {s: [] for s in self.STREAMS}
        self.ndma = {s: 0 for s in self.STREAMS}
        self.slot_last = {}

    def buf(self, name="b", excl=False):
        return Buf(name, excl)

    def _record(self, o, reads, writes, partial):
        deps = {}

        def add(d):
            for k, w in d.items():
                if w is o:
                    continue
                cur = deps.get(k)
                if cur is None or w.idx > cur.idx:
                    deps[k] = w

        for b in reads:
            add(b.writers)
            if b.excl:
                add(b.readers)
        for b in writes:
            add(b.readers)
            if not partial:
                add(b.writers)
        if o.stream == "pe" and not o.is_dma:
            deps = {k: w for k, w in deps.items() if not (w.stream == "pe" and not w.is_dma)}
        o.deps = list(deps.values())
        for b in reads:
            b.readers[o.key] = o
        for b in writes:
            if partial:
                b.writers[o.key] = o
            else:
                b.writers = {o.key: o}
        self.streams[o.stream].append(o)

    def op(self, stream, fn, reads=(), writes=(), partial=False):
        o = Op()
        o.stream = stream
        o.fn = fn
        o.signal = False
        o.val = None
        o.is_dma = False
        o.prev_slot_op = None
        o.idx = len(self.streams[stream])
        o.key = (stream, o.idx // EPOCH)
        self._record(o, reads, writes, partial)
        return o

    def dma(self, stream, out, in_, reads=(), writes=(), partial=False, **kw):
        o = Op()
        o.stream = stream
        o.fn = lambda e: e.dma_start(out=out, in_=in_, **kw)
        o.signal = True
        o.is_dma = True
        o.idx = len(self.streams[stream])
        i = self.ndma[stream]
        self.ndma[stream] += 1
        R = DMA_SLOTS[stream]
        o.key = ("dma", stream, i % R)
        o.val = 16 * (i // R + 1)
        o.prev_slot_op = self.slot_last.get(o.key)
        self.slot_last[o.key] = o
        self._record(o, reads, writes, partial)
        return o

    def emit(self, final_waits=()):
        nc = self.nc
        if final_waits:
            o = Op()
            o.stream = "sp"
            o.fn = None
            o.signal = False
            o.val = None
            o.is_dma = False
            o.prev_slot_op = None
            o.idx = len(self.streams["sp"])
            o.key = ("sp", o.idx // EPOCH)
            o.deps = list(final_waits)
            self.streams["sp"].append(o)
        for s in self.STREAMS:
            for o in self.streams[s]:
                for d in o.deps:
                    d.signal = True
        keys = {}
        for s in self.STREAMS:
            cnt = {}
            for o in self.streams[s]:
                if o.is_dma:
                    keys[o.key] = None
                    continue
                if o.signal:
                    c = cnt.get(o.key, 0) + 1
                    cnt[o.key] = c
                    o.val = c
                    keys[o.key] = None
        stack = contextlib.ExitStack()
        sems = {}
        for k in keys:
            sems[k] = stack.enter_context(nc.semaphore("s_" + "_".join(str(x) for x in k)))
        self.stats = {s: len(self.streams[s]) for s in self.STREAMS}
        self.stats["nsem"] = len(sems)

        def run(s, eng):
            waited = {}
            nw = 0
            for o in self.streams[s]:
                need = {}
                for d in o.deps:
                    if d.val > need.get(d.key, 0):
                        need[d.key] = d.val
                if o.is_dma and o.prev_slot_op is not None:
                    p = o.prev_slot_op
                    if p.val > need.get(p.key, 0):
                        need[p.key] = p.val
                for k, v in need.items():
                    if waited.get(k, 0) < v:
                        eng.wait_ge(sems[k], v)
                        waited[k] = v
                        nw += 1
                if o.fn is None:
                    continue
                ins = o.fn(eng)
                if o.is_dma:
                    ins.then_inc(sems[o.key], 16)
                elif o.signal:
                    ins.then_inc(sems[o.key], 1)
            self.stats["waits_" + s] = nw

        with nc.Block() as block:
            @block.tensor
            def _(e):
                run("pe", e)

            @block.scalar
            def _(e):
                run("act", e)

            @block.vector
            def _(e):
                run("dve", e)

            @block.gpsimd
            def _(e):
                run("pool", e)

            @block.sync
            def _(e):
                run("sp", e)
        stack.close()


C_ID, C_TRI, C_NONE, C_ONE, C_BLK, C_MSB, C_MMO, C_MSW, NCB = 0, 128, 256, 384, 512, 640, 768, 1280, 1536
KINDS = ("sb", "moba", "swa", "sb")
JIDX = (0, 0, 0, 1)


def build_program(nlayers=4, dump=()):
    nc = bass.Bass("TRN2", target_bir_lowering=False)
    P = Prog(nc)
    st = contextlib.ExitStack()

    def din(name, shape, dt=F32):
        return nc.dram_tensor(name, list(shape), dt, kind="ExternalInput").ap()

    def dscr(name, shape, dt):
        kind = "ExternalOutput" if name in dump else "Internal"
        return nc.dram_tensor(name, list(shape), dt, kind=kind).ap()

    xT = din("xT", [D, S])
    cT = din("cT", [128, 8])
    ada_w = din("ada_w", [4, D, 6 * D])
    ada_bT = din("ada_bT", [128, 192])
    gainT = din("gainT", [128, 64])
    w_gate = din("ffn_w_gate", [4, D, DFF])
    w_up = din("ffn_w_up", [4, D, DFF])
    w_down = din("ffn_w_down", [4, DFF, D])
    sb_w_in = din("sb_w_in", [2, D, 3 * D])
    sb_w_out = din("sb_w_out", [2, D, D])
    moba_w_in = din("moba_w_in", [1, D, 3 * D])
    moba_w_sw = din("moba_w_sw", [D, 2 * D])
    moba_w_out = din("moba_w_out", [1, D, D])
    swa_w_in = din("swa_w_in", [1, D, 1280])
    swa_w_sw = din("swa_w_sw", [D, 1152])
    swa_w_out = din("swa_w_out", [1, D, D])
    qkgT = din("qkgT", [128, 8])
    sinkT = din("sinkT", [128, 16])
    cosT = din("cosT", [128, S])
    sinT = din("sinT", [128, S])
    constb = din("constb", [128, NCB], BF16)
    onehot = din("onehot", [16, S], BF16)
    pastmask = din("pastmask", [128, 512])
    yT = nc.dram_tensor("yT", [D, S], F32, kind="ExternalOutput").ap()

    H = dscr("H", [D, S], BF16)
    QT = dscr("QT", [D, S], BF16)
    KT = dscr("KT", [D, S], BF16)
    V = dscr("V", [S, D], BF16)
    OT = dscr("OT", [D, S], BF16)
    U = dscr("U", [D, S], F32)
    DEN = dscr("DEN", [D, S], F32)
    XM = dscr("XM", [D, S], F32)
    XA = dscr("XA", [D, S], F32)
    XB = dscr("XB", [D, S], F32)
    A = dscr("A", [DFF, S], BF16)
    KMT = dscr("KMT", [D, 16], F32)

    dbufs = {}

    def Dd(name, tt):
        k = (name, tt)
        if k not in dbufs:
            dbufs[k] = P.buf(f"{name}{tt}")
        return dbufs[k]

    def Dall(name):
        return [Dd(name, tt) for tt in range(NT)]

    def sb(name, shape, dt):
        return st.enter_context(nc.sbuf_tensor(name, list(shape), dt))

    W = [sb("W0", [128, 22528], BF16), sb("W1", [128, 22528], BF16)]
    Wb = [P.buf("W0"), P.buf("W1")]
    BIG = sb("BIG", [128, 6 * 4096], BF16)
    BIGb = [P.buf(f"BIG{i}") for i in range(6)]
    F16 = [sb("F16a", [128, 4096], F32), sb("F16b", [128, 4096], F32)]
    F16b = [P.buf("F16a"), P.buf("F16b")]
    CB = sb("CB", [128, NCB], BF16)
    CBb = P.buf("CB")
    MOD = sb("MOD", [128, 4 * 48], F32)
    MODb = P.buf("MOD")
    DER = sb("DER", [128, 4 * 16], F32)
    DERb = P.buf("DER")
    GAIN = sb("GAIN", [128, 64], F32)
    ABT = sb("ABT", [128, 192], F32)
    QKG = sb("QKG", [128, 8], F32)
    ESINK = sb("ESINK", [128, 16], F32)
    CACT = sb("CACT", [128, 8], F32)
    MISCb = P.buf("MISC")
    NF = 5
    FT = [sb(f"FT{i}", [128, 512], F32) for i in range(NF)]
    FTb = [P.buf(f"FT{i}") for i in range(NF)]
    NB = 8
    BT = [sb(f"BT{i}", [128, 512], BF16) for i in range(NB)]
    BTb = [P.buf(f"BT{i}") for i in range(NB)]
    RT = [sb(f"RT{i}", [128, 512], BF16) for i in range(2)]
    RTb = [P.buf(f"RT{i}") for i in range(2)]
    AUX = sb("AUX", [128, 4096], BF16)
    VST = [AUX[:, i * 1024:(i + 1) * 1024] for i in range(2)]
    VSTb = [P.buf(f"VST{i}") for i in range(2)]
    CS0b = P.buf("CS0")
    CS = [AUX[:, 2048:4096].bitcast(F32), AUX[:, 0:2048].bitcast(F32)]
    CSb = [[CS0b], [VSTb[0], VSTb[1]]]
    AUXb = [VSTb[0], VSTb[1], CS0b]
    KM = sb("KM", [128, 128], F32)
    KMb = P.buf("KM")
    KMH = sb("KMH", [128, 32], F32)
    KMHb = P.buf("KMH")
    KMHB = sb("KMHB", [128, 16], BF16)
    KMHBb = P.buf("KMHB")
    PMK = sb("PMK", [128, 512], F32)
    PMKb = P.buf("PMK")
    GM = sb("GM", [128, 64], F32)
    GMb = P.buf("GM")
    TOP = sb("TOP", [128, 32], F32)
    TOPb = P.buf("TOP")
    SEL = sb("SEL", [128, 64], F32)
    SELb = P.buf("SEL")
    SPAD = sb("SPAD", [128, 512], BF16)
    SPADb = P.buf("SPAD")
    PSD = [st.enter_context(nc.psum_tensor(f"psd{i}", [128, 1024], F32)) for i in range(4)]
    PS = [PSD[i // 2][:, (i % 2) * 512:(i % 2 + 1) * 512] for i in range(8)]
    PSb = [P.buf(f"ps{i}", excl=True) for i in range(8)]

    rr = {"ps": 0, "ft": 0, "bt": 0, "sps": 0, "ops": 0}

    def bank():
        i = rr["ps"] % 8
        rr["ps"] += 1
        return PS[i], PSb[i]

    def sbank():
        i = rr["sps"] % 6
        rr["sps"] += 1
        return PS[i], PSb[i]

    def obank():
        i = 6 + rr["ops"] % 2
        rr["ops"] += 1
        return PS[i], PSb[i]

    def ft():
        i = rr["ft"] % NF
        rr["ft"] += 1
        return FT[i], FTb[i]

    def bt():
        i = rr["bt"] % NB
        rr["bt"] += 1
        return BT[i], BTb[i]

    def big(u0, n=1):
        return BIG[:, u0 * 4096:(u0 + n) * 4096], BIGb[u0:u0 + n]

    def fm(dram, tt):
        return dram.rearrange("(c p) t -> p c t", p=128)[:, :, tt * 512:(tt + 1) * 512]

    ident = CB[:, C_ID:C_ID + 128]
    negtri = CB[:, C_TRI:C_TRI + 128]
    negones = CB[:, C_NONE:C_NONE + 128]
    ones = CB[:, C_ONE:C_ONE + 128]
    blkones = CB[:, C_BLK:C_BLK + 128]
    mask_sb = CB[:, C_MSB:C_MSB + 128]
    mask_mo = CB[:, C_MMO:C_MMO + 512]
    mask_sw = CB[:, C_MSW:C_MSW + 256]

    P.dma("sp", CB[:], constb, writes=[CBb])
    P.dma("sp", CACT[:], cT, writes=[MISCb], partial=True)
    P.dma("sp", GAIN[:], gainT, writes=[MISCb], partial=True)
    P.dma("sp", ABT[:], ada_bT, writes=[MISCb], partial=True)
    P.dma("sp", QKG[:], qkgT, writes=[MISCb], partial=True)
    P.dma("sp", ESINK[:], sinkT, writes=[MISCb], partial=True)
    P.dma("sp", PMK[:], pastmask, writes=[PMKb])
    P.op("act", lambda e: e.activation(out=CACT[:], in_=CACT[:], func=AF.Silu), reads=[MISCb], writes=[MISCb])
    P.op("act", lambda e: e.activation(out=ESINK[:], in_=ESINK[:], func=AF.Exp), reads=[MISCb], writes=[MISCb])

    ONE = sb("ONE", [128, 1], F32)
    ONEb = P.buf("ONE")
    P.op("pool", lambda e: e.memset(ONE[:], 1.0), writes=[ONEb])

    def mod_view(r):
        return W[r][:, :].bitcast(F32)[:, 0:8 * 512].rearrange("p (k f) -> p k f", k=8)

    def mod_load(l, ct, r):
        wv = mod_view(r)
        src = ada_w[l].rearrange("(k p) f -> p k f", p=128)[:, :, ct * 512:(ct + 1) * 512]
        for k in range(8):
            P.dma("sp", wv[:, k, :], src[:, k, :], writes=[Wb[r]], partial=True)

    def mod_compute(l, ct, r, banks):
        (rps, rpb), (tps, tpb) = banks
        wv = mod_view(r)

        def mm(e):
            ins = None
            for k in range(8):
                ins = e.matmul(rps[0:1, :], lhsT=CACT[:, k:k + 1], rhs=wv[:, k, :], start=(k == 0), stop=(k == 7), skip_group_check=True)
            return ins
        P.op("pe", mm, reads=[Wb[r], MISCb], writes=[rpb])
        row, rowb = ft()
        P.op("dve", lambda e: e.tensor_copy(out=row[0:1, :], in_=rps[0:1, :]), reads=[rpb], writes=[rowb])

        def tr(e):
            ins = None
            for j in range(4):
                ins = e.matmul(tps[:, j:j + 1], lhsT=row[0:1, j * 128:(j + 1) * 128], rhs=ONE[0:1, 0:1],
                               start=True, stop=True, skip_group_check=True)
            return ins
        P.op("pe", tr, reads=[rowb, ONEb], writes=[tpb])
        c0 = l * 48 + ct * 4
        P.op("dve", lambda e: e.tensor_tensor(out=MOD[:, c0:c0 + 4], in0=tps[:, 0:4], in1=ABT[:, c0:c0 + 4], op=ALU.add),
             reads=[tpb, MISCb], writes=[MODb], partial=True)

    def mod_finish(l):
        for j in range(2):
            P.op("dve", lambda e, j=j: e.scalar_tensor_tensor(
                out=DER[:, l * 16 + j * 8:l * 16 + j * 8 + 8], in0=MOD[:, l * 48 + j * 24 + 8:l * 48 + j * 24 + 16],
                scalar=1.0, in1=GAIN[:, l * 16 + j * 8:l * 16 + j * 8 + 8], op0=ALU.add, op1=ALU.mult),
                reads=[MODb, MISCb], writes=[DERb], partial=True)

    for ct in range(12):
        mod_load(0, ct, ct % 2)
        mod_compute(0, ct, ct % 2, (sbank(), sbank()))
    mod_finish(0)
    bg_tasks = []

    def make_bg_tasks(r, getbanks):
        seq = [(l, ct) for l in range(1, nlayers) for ct in range(12)]
        tasks = []
        for k in range(len(seq) + 1):
            def task(k=k):
                if k >= 1:
                    l, ct = seq[k - 1]
                    mod_compute(l, ct, r, getbanks())
                    if ct == 11:
                        mod_finish(l)
                if k < len(seq):
                    mod_load(seq[k][0], seq[k][1], r)
            tasks.append(task)
        return tasks

    def mod(l, which):
        off = {"sh1": 0, "g1": 16, "sh2": 24, "g2": 40}[which]
        return MOD[:, l * 48 + off:l * 48 + off + 8]

    def der(l, j):
        return DER[:, l * 16 + j * 8:l * 16 + j * 8 + 8]

    wstate = {"i": 0}

    def load_w(dram2d, KC, ncols, col0=0, slot=None):
        r = wstate["i"] % 2 if slot is None else slot
        if slot is None:
            wstate["i"] += 1
        view = W[r][:, 0:KC * ncols].rearrange("p (k f) -> p k f", k=KC)
        for k in range(KC):
            c = 0
            while c < ncols:
                n = min(1024, ncols - c)
                P.dma("pool", view[:, k, c:c + n], dram2d[k * 128:(k + 1) * 128, col0 + c:col0 + c + n],
                      writes=[Wb[r]], partial=True)
                c += n
        return view, Wb[r]

    def load_w2(drams, KC, ncols_each):
        r = wstate["i"] % 2
        wstate["i"] += 1
        views = []
        base = 0
        for (dram2d, col0) in drams:
            view = W[r][:, base:base + KC * ncols_each].rearrange("p (k f) -> p k f", k=KC)
            base += KC * ncols_each
            for k in range(KC):
                c = 0
                while c < ncols_each:
                    n = min(1024, ncols_each - c)
                    P.dma("pool", view[:, k, c:c + n], dram2d[k * 128:(k + 1) * 128, col0 + c:col0 + c + n],
                          writes=[Wb[r]], partial=True)
                    c += n
            views.append((view, Wb[r]))
        return views

    def norm_tile(tt, a, b, h2, hb, defer=False):
        xt = F16[tt % 2][:, :].rearrange("p (c t) -> p c t", c=8)
        xb_ = F16b[tt % 2]
        h = h2.rearrange("p (c t) -> p c t", c=8)
        P.op("act", lambda e: e.activation(out=h, in_=xt, func=AF.Square), reads=[xb_], writes=hb)
        if defer:
            return lambda: norm_tile_b(tt, a, b, h2, hb)
        norm_tile_b(tt, a, b, h2, hb)

    def norm_tile_b(tt, a, b, h2, hb):
        xt = F16[tt % 2][:, :].rearrange("p (c t) -> p c t", c=8)
        xb_ = F16b[tt % 2]
        h = h2.rearrange("p (c t) -> p c t", c=8)
        ps, pb = bank()

        def mm(e):
            ins = None
            for c in range(8):
                ins = e.matmul(ps[:], lhsT=ones, rhs=h[:, c, :], start=(c == 0), stop=(c == 7))
            return ins
        P.op("pe", mm, reads=hb + [CBb], writes=[pb])
        r0, r0b = ft()
        P.op("act", lambda e: e.activation(out=r0[:], in_=ps[:], func=AF.Ln, scale=1.0 / D, bias=EPS),
             reads=[pb], writes=[r0b])
        P.op("act", lambda e: e.activation(out=r0[:], in_=r0[:], func=AF.Exp, scale=-0.5),
             reads=[r0b], writes=[r0b])
        for c in range(8):
            P.op("dve", lambda e, c=c: e.scalar_tensor_tensor(
                out=xt[:, c, :], in0=xt[:, c, :], scalar=a[:, c:c + 1], in1=r0[:], op0=ALU.mult, op1=ALU.mult),
                reads=[xb_, r0b, DERb], writes=[xb_], partial=True)
            P.op("act", lambda e, c=c: e.activation(
                out=h[:, c, :], in_=xt[:, c, :], func=AF.Identity, bias=b[:, c:c + 1], scale=1.0),
                reads=[xb_, MODb], writes=hb, partial=True)
        P.dma("sp", fm(H, tt), h, reads=hb, writes=[Dd("H", tt)])

    def norm_phase(src, sname, a, b):
        def ldx(tt):
            P.dma("sp", F16[tt % 2][:, :].rearrange("p (c t) -> p c t", c=8), fm(src, tt), reads=[Dd(sname, tt)], writes=[F16b[tt % 2]])
        ldx(0)
        for tt in range(NT):
            if tt + 1 < NT:
                ldx(tt + 1)
            h2, hb = big(tt % 2)
            norm_tile(tt, a, b, h2, hb)

    def proj_phase(src, sname, KC, wsets, n_out, epi, pre=None, post=None, pre_pf=None):
        def ld(tt):
            if KC == 8:
                r2, rb = big(tt % 2)
            else:
                r2, rb = big(3 * (tt % 2), 3)
                r2 = r2[:, 0:KC * 512]
            rhs = r2.rearrange("p (k t) -> p k t", k=KC)
            P.dma("sp", rhs, fm(src, tt), reads=[Dd(sname, tt)], writes=rb)
            if pre_pf:
                pre_pf(tt)
            return rhs, rb
        nxt = ld(0)
        postdef = [None]
        for tt in range(NT):
            rhs, rb = nxt
            loaded_next = False
            if pre:
                pre(tt)
            pending = None
            for o in range(n_out):
                if o == 1:
                    if postdef[0]:
                        postdef[0]()
                        postdef[0] = None
                    if tt + 1 < NT:
                        nxt = ld(tt + 1)
                        loaded_next = True
                banks = []
                for (Wv, Wbb) in wsets:
                    ps, pb = bank()

                    def mm(e, ps=ps, Wv=Wv, rhs=rhs, o=o):
                        ins = None
                        for k in range(KC):
                            ins = e.matmul(ps[:], lhsT=Wv[:, k, o * 128:(o + 1) * 128], rhs=rhs[:, k, :],
                                           start=(k == 0), stop=(k == KC - 1))
                        return ins
                    P.op("pe", mm, reads=rb + [Wbb], writes=[pb])
                    banks.append((ps, pb))
                if pending:
                    pending()
                pending = epi(tt, o, banks)
            if pending:
                pending()
            if postdef[0]:
                postdef[0]()
                postdef[0] = None
            if not loaded_next and tt + 1 < NT:
                nxt = ld(tt + 1)
            if post:
                postdef[0] = post(tt)
        if postdef[0]:
            postdef[0]()

    def epi_copy(dst, dname, row0, scale):
        def f(tt, o, banks):
            ps, pb = banks[0]
            t, tb = bt()
            if o % 2 == 0:
                P.op("dve", lambda e: e.tensor_scalar(out=t[:], in0=ps[:], scalar1=scale, scalar2=None, op0=ALU.mult),
                     reads=[pb], writes=[tb])
            else:
                P.op("act", lambda e: e.activation(out=t[:], in_=ps[:], func=AF.Copy, scale=scale), reads=[pb], writes=[tb])
            P.dma("sp", dst[row0 + o * 128:row0 + (o + 1) * 128, tt * 512:(tt + 1) * 512], t[:],
                  reads=[tb], writes=[Dd(dname, tt)], partial=True)
        return f

    def rope_pre(tt):
        cs = CS[tt % 2]
        P.dma("sp", cs[:, 0:512], cosT[:, tt * 512:(tt + 1) * 512], writes=CSb[tt % 2])
        P.dma("sp", cs[:, 512:1024], sinT[:, tt * 512:(tt + 1) * 512], writes=CSb[tt % 2], partial=True)

    def epi_rope(dst, dname, row0, gcol, qscale, kmean):
        lnb = float(np.log(qscale))

        def f(tt, o, banks):
            (q, qb), (qs, qsb) = banks
            cs = CS[tt % 2]
            csb = CSb[tt % 2]
            sq, sqb = bt()
            P.op("act", lambda e: e.activation(out=sq[:], in_=q[:], func=AF.Square), reads=[qb], writes=[sqb])
            t1, t1b = ft()
            t2, t2b = ft()
            P.op("dve", lambda e: e.scalar_tensor_tensor(out=t1[:], in0=q[:], scalar=QKG[:, gcol:gcol + 1], in1=cs[:, 0:512],
                                                         op0=ALU.mult, op1=ALU.mult), reads=[qb, MISCb] + csb, writes=[t1b])
            P.op("dve", lambda e: e.scalar_tensor_tensor(out=t2[:], in0=qs[:], scalar=QKG[:, gcol + 1:gcol + 2], in1=cs[:, 512:1024],
                                                         op0=ALU.mult, op1=ALU.mult), reads=[qsb, MISCb] + csb, writes=[t2b])
            P.op("pool", lambda e: e.tensor_tensor(out=t1[:], in0=t1[:], in1=t2[:], op=ALU.add), reads=[t1b, t2b], writes=[t1b])

            def deferred():
                ss, ssb = bank()
                P.op("pe", lambda e: e.matmul(ss[:], lhsT=blkones, rhs=sq[:], start=True, stop=True), reads=[sqb, CBb], writes=[ssb])
                rs, rsb = ft()
                P.op("act", lambda e: e.activation(out=rs[:], in_=ss[:], func=AF.Ln, scale=1.0 / 64, bias=EPS), reads=[ssb], writes=[rsb])
                if lnb == 0.0:
                    P.op("act", lambda e: e.activation(out=rs[:], in_=rs[:], func=AF.Exp, scale=-0.5), reads=[rsb], writes=[rsb])
                else:
                    P.op("act", lambda e: e.activation(out=rs[:], in_=rs[:], func=AF.Exp, scale=-0.5, bias=lnb), reads=[rsb], writes=[rsb])
                ob, obb = bt()
                P.op("pool", lambda e: e.tensor_tensor(out=ob[:], in0=t1[:], in1=rs[:], op=ALU.mult), reads=[t1b, rsb], writes=[obb])
                if kmean:
                    P.op("dve", lambda e: e.tensor_reduce(
                        out=KM[:, o * 16 + tt * 2:o * 16 + tt * 2 + 2], in_=ob[:, :].rearrange("p (n s) -> p n s", s=256),
                        axis=AX.X, op=ALU.add), reads=[obb], writes=[KMb], partial=True)
                P.dma("sp", dst[row0 + o * 128:row0 + (o + 1) * 128, tt * 512:(tt + 1) * 512], ob[:],
                      reads=[obb], writes=[Dd(dname, tt)], partial=True)
            return deferred
        return f

    def resid_pre(src, sname):
        def f(tt):
            xt = F16[tt % 2][:, :].rearrange("p (c t) -> p c t", c=8)
            P.dma("sp", xt, fm(src, tt), reads=[Dd(sname, tt)], writes=[F16b[tt % 2]])
        return f

    def epi_resid(g):
        def f(tt, o, banks):
            ps, pb = banks[0]
            xt = F16[tt % 2][:, :].rearrange("p (c t) -> p c t", c=8)
            P.op("dve", lambda e: e.scalar_tensor_tensor(out=xt[:, o, :], in0=ps[:], scalar=g[:, o:o + 1], in1=xt[:, o, :],
                                                         op0=ALU.mult, op1=ALU.add),
                 reads=[pb, F16b[tt % 2], MODb], writes=[F16b[tt % 2]], partial=True)
        return f

    def resid_post(dst, dname, outs=None, norm=None):
        def f(tt):
            xt = F16[tt % 2][:, :].rearrange("p (c t) -> p c t", c=8)
            o = P.dma("sp", fm(dst, tt), xt, reads=[F16b[tt % 2]], writes=[Dd(dname, tt)])
            if outs is not None:
                outs.append(o)
            if norm is not None:
                a, b, hsel = norm
                h2, hb = hsel(tt)
                return norm_tile(tt, a, b, h2, hb, defer=True)
        return f

    def epi_swiglu(half):
        def f(tt, o, banks):
            (g, gb), (u, ub) = banks
            a2, ab = big(2 + 2 * (tt % 2), 2)
            av = a2[:, 0:11 * 512].rearrange("p (c t) -> p c t", c=11)
            sg, sgb = ft()
            P.op("act", lambda e: e.activation(out=sg[:], in_=g[:], func=AF.Silu), reads=[gb], writes=[sgb])
            P.op("dve", lambda e: e.tensor_tensor(out=av[:, o, :], in0=sg[:], in1=u[:], op=ALU.mult),
                 reads=[sgb, ub], writes=ab, partial=True)
        return f

    def swiglu_post(half):
        def f(tt):
            a2, ab = big(2 + 2 * (tt % 2), 2)
            av = a2[:, 0:11 * 512].rearrange("p (c t) -> p c t", c=11)
            dst = A[half * 1408:(half + 1) * 1408, :].rearrange("(c p) t -> p c t", p=128)[:, :, tt * 512:(tt + 1) * 512]
            P.dma("sp", dst, av, reads=ab, writes=[Dd("A", tt)], partial=True)
        return f

    def vproj_phase(Wv, Wvb, ncols, vrows_cols):
        def ld(tt):
            r2, rb = big(tt % 2)
            rhs = r2.rearrange("p (k t) -> p k t", k=8)
            P.dma("sp", rhs, fm(H, tt), reads=[Dd("H", tt)], writes=rb)
            return rhs, rb
        nxt = ld(0)
        for tt in range(NT):
            rhs, rb = nxt
            if tt + 1 < NT:
                nxt = ld(tt + 1)
            for j in range(4):
                vs = VST[j % 2]
                vsb = VSTb[j % 2]
                nh = (ncols + 511) // 512
                for hf in range(nh):
                    n = min(512, ncols - hf * 512)
                    ps, pb = bank()

                    def mm(e, ps=ps, rhs=rhs, j=j, hf=hf, n=n):
                        ins = None
                        for k in range(8):
                            ins = e.matmul(ps[:, 0:n], lhsT=rhs[:, k, j * 128:(j + 1) * 128], rhs=Wv[:, k, hf * 512:hf * 512 + n],
                                           start=(k == 0), stop=(k == 7))
                        return ins
                    P.op("pe", mm, reads=rb + [Wvb], writes=[pb])
                    if hf % 2 == 0:
                        P.op("act", lambda e, ps=ps, vs=vs, hf=hf, n=n: e.activation(out=vs[:, hf * 512:hf * 512 + n], in_=ps[:, 0:n], func=AF.Copy),
                             reads=[pb], writes=[vsb], partial=True)
                    else:
                        P.op("dve", lambda e, ps=ps, vs=vs, hf=hf, n=n: e.tensor_copy(out=vs[:, hf * 512:hf * 512 + n], in_=ps[:, 0:n]),
                             reads=[pb], writes=[vsb], partial=True)
                P.dma("sp", V[tt * 512 + j * 128:tt * 512 + (j + 1) * 128, 0:ncols], vs[:, 0:ncols],
                      reads=[vsb], writes=[Dd("V", tt)], partial=True)

    def attn_sb(bg=None):
        SUB = []
        for x in range(2):
            P.op("pool", lambda e, x=x: e.memset(F16[x][:, 0:2], 0.0), writes=[F16b[x]])
            fb = F16[x][:, :].bitcast(BF16)
            for i in range(8):
                SUB.append((fb[:, i * 1024:(i + 1) * 1024].rearrange("p (h n) -> p h n", h=2), P.buf(f"sub{x}{i}"), F16b[x]))
        E2, SP2, A2, R2 = SUB[0:3], SUB[3:7], SUB[7:10], [SUB[10:12], SUB[12:14]]
        rot = {"e": 0, "sp": 0, "a": 0, "s": 0}

        def nxt(lst, key):
            t = lst[rot[key] % len(lst)]
            rot[key] += 1
            return t

        def sdb():
            k = rot["s"] % 3
            rot["s"] += 1
            return PSD[k][:, :].rearrange("p (h n) -> p h n", h=2), [PSb[2 * k], PSb[2 * k + 1]]

        def bg_banks():
            k = rot["s"] % 3
            rot["s"] += 1
            return (PS[2 * k], PSb[2 * k]), (PS[2 * k + 1], PSb[2 * k + 1])
        if bg is not None and not bg:
            bg.extend(make_bg_tasks(wstate["i"] % 2, bg_banks))

        def load_pair(j):
            b = j % 2
            kt, ktb = big(3 * b)
            qt, qtb = big(3 * b + 1)
            vt, vtb = big(3 * b + 2)
            P.dma("sp", kt, KT[j * 128:(j + 1) * 128, :], reads=Dall("KT"), writes=ktb)
            P.dma("sp", qt, QT[j * 128:(j + 1) * 128, :], reads=Dall("QT"), writes=qtb)
            vv = vt.rearrange("p (n f) -> p n f", f=128)
            vsrc = V.rearrange("(n p) f -> p n f", p=128)[:, :, j * 128:(j + 1) * 128]
            for q4 in range(4):
                P.dma("sp", vv[:, q4 * 8:(q4 + 1) * 8, :], vsrc[:, q4 * 8:(q4 + 1) * 8, :], reads=Dall("V"), writes=vtb, partial=True)

        units = []
        for j in range(8):
            for i in range(8):
                for r, kb in enumerate(range(4 * i + 3, -1, -1)):
                    units.append(dict(j=j, i=i, kb=kb, r=r, c0=max(0, kb - 4 * i) * 128, diag=(kb >= 4 * i),
                                      first=(r == 0), last=(kb == 0), newpair=(i == 0 and r == 0)))
        load_pair(0)
        cur = {}

        def s1(u):
            j, i, kb, c0 = u["j"], u["i"], u["kb"], u["c0"]
            b = j % 2
            if u["first"]:
                cur["ob"] = obank()
                for (r2, r2b, r2g) in R2[i % 2]:
                    P.op("pool", lambda e, r2=r2: e.memset(r2, 0.0), reads=[r2g], writes=[r2b])
            u["ob"] = cur["ob"]
            kt, ktb = big(3 * b)
            qt, qtb = big(3 * b + 1)
            S3v, sb_ = sdb()
            u["S"], u["Sb"] = S3v, sb_

            def mm(e):
                ins = None
                for hh in (0, 1):
                    ins = e.matmul(S3v[:, hh, c0:512], lhsT=kt[hh * 64:(hh + 1) * 64, kb * 128:(kb + 1) * 128],
                                   rhs=qt[hh * 64:(hh + 1) * 64, i * 512 + c0:(i + 1) * 512], start=True, stop=False,
                                   skip_group_check=True)
                if u["diag"]:
                    for hh in (0, 1):
                        ins = e.matmul(S3v[:, hh, c0:c0 + 128], lhsT=ident, rhs=mask_sb, start=False, stop=False, skip_group_check=True)
                return ins
            P.op("pe", mm, reads=ktb + qtb + [CBb], writes=sb_)
            et, etb, etg = nxt(E2, "e")
            spt, sptb, sptg = nxt(SP2, "sp")
            u["sp"], u["spb"], u["spg"] = spt, sptb, sptg
            P.op("act", lambda e: e.activation(out=et[:, :, c0:512], in_=S3v[:, :, c0:512], func=AF.Exp), reads=sb_ + [etg], writes=[etb])
            u["e"], u["eb"] = et, etb

        def s1b(u):
            c0 = u["c0"]
            et, etb = u["e"], u["eb"]
            spt, sptb, sptg = u["sp"], u["spb"], u["spg"]
            P.op("act", lambda e: e.activation(out=spt[:, :, c0:512], in_=et[:, :, c0:512], func=AF.Ln, bias=1.0),
                 reads=[etb, sptg], writes=[sptb])

        def s3(u):
            i, c0 = u["i"], u["c0"]
            S3v, sb_, spt, sptb, sptg = u["S"], u["Sb"], u["sp"], u["spb"], u["spg"]
            r2, r2b, r2g = R2[i % 2][u["r"] % 2]
            rn, rnb, rng = R2[i % 2][(u["r"] + 1) % 2]

            def mm(e):
                ins = None
                for hh in (0, 1):
                    ins = e.matmul(S3v[:, hh, c0:512], lhsT=negtri, rhs=spt[:, hh, c0:512], start=False, stop=u["first"], skip_group_check=True)
                if not u["first"]:
                    for hh in (0, 1):
                        ins = e.matmul(S3v[:, hh, c0:512], lhsT=negones, rhs=r2[:, hh, c0:512], start=False, stop=True, skip_group_check=True)
                return ins
            P.op("pe", mm, reads=[sptb, CBb, r2b], writes=sb_)
            at, atb, atg = nxt(A2, "a")
            u["a"], u["ab"] = at, atb
            P.op("act", lambda e: e.activation(out=at[:, :, c0:512], in_=S3v[:, :, c0:512], func=AF.Exp), reads=sb_ + [atg], writes=[atb])
            if not u["last"]:
                P.op("dve", lambda e: e.tensor_tensor(out=rn[:, :, c0:512], in0=r2[:, :, c0:512], in1=spt[:, :, c0:512], op=ALU.add),
                     reads=[sptb, r2b, rng], writes=[rnb])

        def s5(u):
            j, i, kb, c0 = u["j"], u["i"], u["kb"], u["c0"]
            b = j % 2
            vt, vtb = big(3 * b + 2)
            vv = vt.rearrange("p (n f) -> p n f", f=128)
            ops_, opb = u["ob"]
            at, atb = u["a"], u["ab"]

            def mm(e):
                ins = None
                for hh in (0, 1):
                    ins = e.matmul(ops_[hh * 64:(hh + 1) * 64, c0:512], lhsT=vv[:, kb, hh * 64:(hh + 1) * 64], rhs=at[:, hh, c0:512],
                                   start=u["first"], stop=u["last"], skip_group_check=True)
                return ins
            P.op("pe", mm, reads=vtb + [atb], writes=[opb], partial=True)
            if u["last"]:
                t, tb = bt()
                if i % 2 == 0:
                    P.op("act", lambda e: e.activation(out=t[:], in_=ops_[:], func=AF.Copy), reads=[opb], writes=[tb])
                else:
                    P.op("dve", lambda e: e.tensor_copy(out=t[:], in_=ops_[:]), reads=[opb], writes=[tb])
                P.dma("sp", OT[j * 128:(j + 1) * 128, i * 512:(i + 1) * 512], t[:], reads=[tb], writes=[Dd("OT", i)], partial=True)

        N = len(units)
        for idx in range(N + 2):
            if idx < N:
                s1(units[idx])
            if 0 <= idx - 1 < N:
                s3(units[idx - 1])
            if idx < N:
                s1b(units[idx])
            if 0 <= idx - 2 < N:
                s5(units[idx - 2])
            if 0 <= idx - 1 < N and units[idx - 1]["newpair"] and units[idx - 1]["j"] + 1 < 8:
                load_pair(units[idx - 1]["j"] + 1)
            if bg and idx % 28 == 10:
                bg.pop(0)()
        while bg:
            bg.pop(0)()

    def attn_moba():
        for b in range(2):
            vt, vtb = big(3 * b + 2)
            vv = vt.rearrange("p (n f) -> p n f", f=128)
            P.op("pool", lambda e, vv=vv: e.memset(vv[:, :, 64:128], 1.0), writes=vtb)
        P.op("pool", lambda e: e.memset(SPAD[:], 0.0), writes=[SPADb])
        P.dma("sp", KMT.rearrange("(c p) n -> p c n", p=128), KM[:, :].rearrange("p (c n) -> p c n", c=8), reads=[KMb], writes=[Dd("KMT", 0)])

        def load_head(h):
            b = h % 2
            ka, kab = big(3 * b)
            qa, qab = big(3 * b + 1)
            vt, vtb = big(3 * b + 2)
            vv = vt.rearrange("p (n f) -> p n f", f=128)
            P.dma("sp", ka[0:64, :], KT[h * 64:(h + 1) * 64, :], reads=Dall("KT"), writes=kab)
            P.dma("sp", ka[64:80, :], onehot, writes=kab, partial=True)
            P.dma("sp", qa[0:64, :], QT[h * 64:(h + 1) * 64, :], reads=Dall("QT"), writes=qab)
            vsrc = V.rearrange("(n p) f -> p n f", p=128)[:, :, h * 64:(h + 1) * 64]
            for q4 in range(4):
                P.dma("sp", vv[:, q4 * 8:(q4 + 1) * 8, 0:64], vsrc[:, q4 * 8:(q4 + 1) * 8, :], reads=Dall("V"), writes=vtb, partial=True)

        def pp_init(h):
            kmh = KMH[0:64, (h % 2) * 16:(h % 2) * 16 + 16]
            P.dma("sp", kmh, KMT[h * 64:(h + 1) * 64, :], reads=[Dd("KMT", 0)], writes=[KMHb])
            P.op("dve", lambda e: e.tensor_copy(out=KMHB[0:64, :], in_=kmh), reads=[KMHb], writes=[KMHBb])

        def pp_a(h, g):
            b = h % 2
            qa, qab = big(3 * b + 1)
            gps, gpb = sbank()

            def mm(e):
                ins = None
                for tq in range(4):
                    t0 = (g * 4 + tq) * 128
                    ins = e.matmul(gps[:, tq * 16:(tq + 1) * 16], lhsT=qa[0:64, t0:t0 + 128], rhs=KMHB[0:64, :], start=True, stop=True,
                                   skip_group_check=True)
                return ins
            P.op("pe", mm, reads=qab + [KMHBb], writes=[gpb])
            P.op("dve", lambda e: e.tensor_tensor(out=GM[:], in0=gps[:, 0:64], in1=PMK[:, g * 64:(g + 1) * 64], op=ALU.add),
                 reads=[gpb, PMKb], writes=[GMb])
            for tq in range(4):
                P.op("dve", lambda e, tq=tq: e.max(out=TOP[:, tq * 8:(tq + 1) * 8], in_=GM[:, tq * 16:(tq + 1) * 16]),
                     reads=[GMb], writes=[TOPb], partial=True)
            gm3 = GM[:, :].rearrange("p (q n) -> p q n", n=16)
            top3 = TOP[:, :].rearrange("p (q n) -> p q n", n=8)
            sel3 = SEL[:, :].rearrange("p (q n) -> p q n", n=16)
            P.op("dve", lambda e: e.tensor_tensor(out=sel3, in0=gm3, in1=top3[:, :, 2:3].to_broadcast([128, 4, 16]), op=ALU.is_lt),
                 reads=[GMb, TOPb], writes=[SELb])
            sp3 = SPAD[:, :].rearrange("p (q n) -> p q n", n=128)
            P.op("dve", lambda e: e.tensor_scalar(out=sp3[:, :, 64:80], in0=sel3, scalar1=NEG, scalar2=None, op0=ALU.mult),
                 reads=[SELb], writes=[SPADb])

        def pp_b(h, g):
            b = h % 2
            qa, qab = big(3 * b + 1)
            sp3 = SPAD[:, :].rearrange("p (q n) -> p q n", n=128)
            tps, tpb = sbank()

            def mm2(e):
                ins = None
                for tq in range(4):
                    ins = e.matmul(tps[:, tq * 128:(tq + 1) * 128], lhsT=sp3[:, tq, :], rhs=ident, start=True, stop=True, skip_group_check=True)
                return ins
            P.op("pe", mm2, reads=[SPADb, CBb], writes=[tpb])
            P.op("act", lambda e: e.activation(out=qa[64:80, g * 512:(g + 1) * 512], in_=tps[64:80, :], func=AF.Copy),
                 reads=[tpb], writes=qab, partial=True)

        def prepass(h):
            pp_init(h)
            for g in range(8):
                pp_a(h, g)
                pp_b(h, g)

        units = []
        for h in range(16):
            for qi in range(8):
                for kb in range(4 * qi + 4):
                    units.append(dict(h=h, qi=qi, kb=kb, r=kb - 4 * qi, first=(kb == 0), last=(kb == 4 * qi + 3),
                                      newhead=(qi == 0 and kb == 0), k=len(units) % 144))
        load_head(0)
        prepass(0)
        cur = {}

        def s1(u):
            h, qi, kb, r = u["h"], u["qi"], u["kb"], u["r"]
            b = h % 2
            if h + 1 < 16:
                k = u["k"]
                if k >= 30 and (k - 30) % 6 == 0 and (k - 30) // 6 <= 8:
                    g = (k - 30) // 6
                    if g == 0:
                        pp_init(h + 1)
                    if g >= 1:
                        pp_b(h + 1, g - 1)
                    if g <= 7:
                        pp_a(h + 1, g)
            if u["first"]:
                cur["ob"] = obank()
            u["ob"] = cur["ob"]
            ka, kab = big(3 * b)
            qa, qab = big(3 * b + 1)
            ps, pb = sbank()
            u["ps"], u["pb"] = ps, pb
            q0 = qi * 512
            kc = slice(kb * 128, (kb + 1) * 128)
            if r < 0:
                c0 = 0
                P.op("pe", lambda e: e.matmul(ps[:, 0:512], lhsT=ka[0:80, kc], rhs=qa[0:80, q0:q0 + 512],
                                              start=True, stop=True, skip_group_check=True), reads=kab + qab, writes=[pb])
            elif r < 2:
                c0 = 0

                def mm(e):
                    e.matmul(ps[:, 0:256], lhsT=ka[0:64, kc], rhs=qa[0:64, q0:q0 + 256], start=True, stop=False, skip_group_check=True)
                    e.matmul(ps[:, 0:256], lhsT=ident, rhs=mask_mo[:, r * 256:(r + 1) * 256], start=False, stop=False, skip_group_check=True)
                    return e.matmul(ps[:, 256:512], lhsT=ka[0:80, kc], rhs=qa[0:80, q0 + 256:q0 + 512], start=False, stop=True,
                                    skip_group_check=True)
                P.op("pe", mm, reads=kab + qab + [CBb], writes=[pb])
            else:
                c0 = 256

                def mm(e):
                    e.matmul(ps[:, 256:512], lhsT=ka[0:64, kc], rhs=qa[0:64, q0 + 256:q0 + 512], start=True, stop=False, skip_group_check=True)
                    return e.matmul(ps[:, 256:512], lhsT=ident, rhs=mask_mo[:, (r - 2) * 256:(r - 1) * 256], start=False, stop=True,
                                    skip_group_check=True)
                P.op("pe", mm, reads=kab + qab + [CBb], writes=[pb])
            u["c0"] = c0
            pt, ptb = bt()
            u["p"], u["pbuf"] = pt, ptb
            P.op("act", lambda e: e.activation(out=pt[:, c0:512], in_=ps[:, c0:512], func=AF.Exp), reads=[pb], writes=[ptb])

        def s3(u):
            h, qi, kb, c0 = u["h"], u["qi"], u["kb"], u["c0"]
            b = h % 2
            vt, vtb = big(3 * b + 2)
            vv = vt.rearrange("p (n f) -> p n f", f=128)
            ops_, opb = u["ob"]
            pt, ptb = u["p"], u["pbuf"]
            P.op("pe", lambda e: e.matmul(ops_[:, c0:512], lhsT=vv[:, kb, :], rhs=pt[:, c0:512], start=u["first"], stop=u["last"],
                                          skip_group_check=True), reads=vtb + [ptb], writes=[opb], partial=True)
            if u["last"]:
                t, tb = ft()
                P.op("act", lambda e: e.activation(out=t[0:64, :], in_=ops_[64:128, :], func=AF.Ln), reads=[opb], writes=[tb])
                P.op("act", lambda e: e.activation(out=t[0:64, :], in_=t[0:64, :], func=AF.Exp, scale=-1.0), reads=[tb], writes=[tb])
                ot, otb = bt()
                P.op("dve", lambda e: e.tensor_tensor(out=ot[0:64, :], in0=ops_[0:64, :], in1=t[0:64, :], op=ALU.mult),
                     reads=[opb, tb], writes=[otb])
                P.dma("sp", OT[h * 64:(h + 1) * 64, qi * 512:(qi + 1) * 512], ot[0:64, :], reads=[otb], writes=[Dd("OT", qi)], partial=True)

        N = len(units)
        SK = 3
        for idx in range(N + SK):
            if idx < N:
                s1(units[idx])
            if 0 <= idx - SK < N:
                s3(units[idx - SK])
            k = idx - SK + 1
            if 0 <= k < N and units[k]["newhead"] and units[k]["h"] + 1 < 16:
                load_head(units[k]["h"] + 1)

    def attn_swa():
        for b in range(2):
            vt, vtb = big(3 * b + 2)
            vv = vt.rearrange("p (n f) -> p n f", f=128)
            P.op("pool", lambda e, vv=vv: e.memset(vv[:, :, 64:128], 1.0), writes=vtb)

        def load_pair(j):
            b = j % 2
            g = j // 4
            qp, qpb = big(3 * b)
            k2, k2b = big(3 * b + 1)
            vt, vtb = big(3 * b + 2)
            vv = vt.rearrange("p (n f) -> p n f", f=128)
            P.dma("sp", qp, QT[j * 128:(j + 1) * 128, :], reads=Dall("QT"), writes=qpb)
            P.dma("sp", k2[0:64, :], KT[g * 64:(g + 1) * 64, :], reads=Dall("KT"), writes=k2b)
            P.dma("sp", k2[64:128, :], KT[g * 64:(g + 1) * 64, :], reads=Dall("KT"), writes=k2b, partial=True)
            vsrc = V.rearrange("(n p) f -> p n f", p=128)[:, :, g * 64:(g + 1) * 64]
            for q4 in range(4):
                P.dma("sp", vv[:, q4 * 8:(q4 + 1) * 8, 0:64], vsrc[:, q4 * 8:(q4 + 1) * 8, :], reads=Dall("V"), writes=vtb, partial=True)

        units = []
        for j in range(8):
            for hh in (0, 1):
                for n in range(32):
                    units.append(dict(j=j, hh=hh, n=n, newpair=(hh == 0 and n == 0)))
        load_pair(0)
        cur = {}

        def s1(u):
            j, hh, n = u["j"], u["hh"], u["n"]
            b = j % 2
            if n % 4 == 0:
                cur["ob"] = obank()
            u["ob"] = cur["ob"]
            qp, qpb = big(3 * b)
            k2, k2b = big(3 * b + 1)
            ps, pb = sbank()
            u["ps"], u["pb"] = ps, pb
            rows = slice(hh * 64, (hh + 1) * 64)
            c0 = 128 if n == 0 else 0
            u["c0"] = c0

            def mm(e):
                if n > 0:
                    e.matmul(ps[:, 0:128], lhsT=k2[rows, (n - 1) * 128:n * 128], rhs=qp[rows, n * 128:(n + 1) * 128], start=True, stop=False,
                             skip_group_check=True)
                e.matmul(ps[:, 128:256], lhsT=k2[rows, n * 128:(n + 1) * 128], rhs=qp[rows, n * 128:(n + 1) * 128], start=(n == 0), stop=False,
                         skip_group_check=True)
                return e.matmul(ps[:, c0:256], lhsT=ident, rhs=mask_sw[:, c0:256], start=False, stop=True, skip_group_check=True)
            P.op("pe", mm, reads=qpb + k2b + [CBb], writes=[pb])
            pt, ptb = bt()
            u["p"], u["pbuf"] = pt, ptb
            P.op("act", lambda e: e.activation(out=pt[:, c0:256], in_=ps[:, c0:256], func=AF.Exp), reads=[pb], writes=[ptb])

        def s3(u):
            j, hh, n, c0 = u["j"], u["hh"], u["n"], u["c0"]
            b = j % 2
            h = 2 * j + hh
            vt, vtb = big(3 * b + 2)
            vv = vt.rearrange("p (n f) -> p n f", f=128)
            ops_, opb = u["ob"]
            pt, ptb = u["p"], u["pbuf"]
            cc = (n % 4) * 128

            def mm(e):
                if n > 0:
                    e.matmul(ops_[:, cc:cc + 128], lhsT=vv[:, n - 1, :], rhs=pt[:, 0:128], start=True, stop=False, skip_group_check=True)
                return e.matmul(ops_[:, cc:cc + 128], lhsT=vv[:, n, :], rhs=pt[:, 128:256], start=(n == 0), stop=True, skip_group_check=True)
            P.op("pe", mm, reads=vtb + [ptb], writes=[opb], partial=True)
            if n % 4 == 3:
                t, tb = ft()
                P.op("act", lambda e: e.activation(out=t[0:64, :], in_=ops_[64:128, :], func=AF.Ln, bias=ESINK[64:128, h:h + 1], scale=1.0),
                     reads=[opb, MISCb], writes=[tb])
                P.op("act", lambda e: e.activation(out=t[0:64, :], in_=t[0:64, :], func=AF.Exp, scale=-1.0), reads=[tb], writes=[tb])
                ot, otb = bt()
                P.op("dve", lambda e: e.tensor_tensor(out=ot[0:64, :], in0=ops_[0:64, :], in1=t[0:64, :], op=ALU.mult),
                     reads=[opb, tb], writes=[otb])
                c = (n - 3) * 128
                P.dma("sp", OT[h * 64:(h + 1) * 64, c:c + 512], ot[0:64, :], reads=[otb], writes=[Dd("OT", c // 512)], partial=True)

        N = len(units)
        SK = 3
        for idx in range(N + SK):
            if idx < N:
                s1(units[idx])
            if 0 <= idx - SK < N:
                s3(units[idx - SK])
            k = idx - SK + 1
            if 0 <= k < N and units[k]["newpair"] and units[k]["j"] + 1 < 8:
                load_pair(units[k]["j"] + 1)

    def normalize_phase():
        it2 = 0
        for tt in range(NT):
            for hf in range(2):
                cols = slice(tt * 512 + hf * 256, tt * 512 + (hf + 1) * 256)
                uu = F16[0][:, hf * 2048:(hf + 1) * 2048].rearrange("p (c t) -> p c t", c=8)
                dd = F16[1][:, hf * 2048:(hf + 1) * 2048].rearrange("p (c t) -> p c t", c=8)
                usrc = U.rearrange("(c p) t -> p c t", p=128)[:, :, cols]
                dsrc = DEN.rearrange("(c p) t -> p c t", p=128)[:, :, cols]
                P.dma("sp", uu, usrc, reads=[Dd("U", tt)], writes=[F16b[0]], partial=True)
                P.dma("sp", dd, dsrc, reads=[Dd("DEN", tt)], writes=[F16b[1]], partial=True)
                P.op("dve", lambda e, dd=dd: e.reciprocal(out=dd, in_=dd), reads=[F16b[1]], writes=[F16b[1]])
                o2, ob = big(it2 % 2)
                it2 += 1
                ov = o2[:, 0:2048].rearrange("p (c t) -> p c t", c=8)
                P.op("pool", lambda e, uu=uu, dd=dd, ov=ov: e.tensor_tensor(out=ov, in0=uu, in1=dd, op=ALU.mult),
                     reads=[F16b[0], F16b[1]], writes=ob)
                P.dma("sp", OT.rearrange("(c p) t -> p c t", p=128)[:, :, cols], ov, reads=ob, writes=[Dd("OT", tt)], partial=True)

    outs = []
    xs_names = ["XA", "XB"]
    xs = {"XA": XA, "XB": XB}
    cur_x, cur_name = xT, "xT"
    for l in range(nlayers):
        kind = KINDS[l]
        jl = JIDX[l]
        a1, b1, g1 = der(l, 0), mod(l, "sh1"), mod(l, "g1")
        a2, b2, g2 = der(l, 1), mod(l, "sh2"), mod(l, "g2")
        if l == 0:
            norm_phase(cur_x, cur_name, a1, b1)
        if kind == "sb":
            w_in = sb_w_in[jl]
            wq = load_w(w_in, 8, 2048)
            wv_ = load_w(w_in, 8, 1024, col0=2048)

            def epi_qk(tt, o, banks, fq=epi_copy(QT, "QT", 0, 0.125), fk=epi_copy(KT, "KT", 0, 1.0)):
                if o < 8:
                    fq(tt, o, banks)
                else:
                    fk(tt, o - 8, banks)
            proj_phase(H, "H", 8, [wq], 16, epi_qk)
            wo = load_w(sb_w_out[jl], 8, 1024)
            vproj_phase(wv_[0], wv_[1], 1024, None)
            attn_sb(bg=bg_tasks if (l == 0 and nlayers > 1) else None)
        elif kind == "moba":
            w_in = moba_w_in[0]
            wq = load_w2([(w_in, 0), (moba_w_sw, 0)], 8, 1024)
            wk = load_w2([(w_in, 1024), (moba_w_sw, 1024)], 8, 1024)
            proj_phase(H, "H", 8, wq, 8, epi_rope(QT, "QT", 0, 0, 0.125, False), pre_pf=rope_pre)
            wv_ = load_w(w_in, 8, 1024, col0=2048)
            proj_phase(H, "H", 8, wk, 8, epi_rope(KT, "KT", 0, 2, 1.0, True), pre_pf=rope_pre)
            wo = load_w(moba_w_out[0], 8, 1024)
            vproj_phase(wv_[0], wv_[1], 1024, None)
            attn_moba()
        else:
            w_in = swa_w_in[0]
            wq = load_w2([(w_in, 0), (swa_w_sw, 0)], 8, 1024)
            wk = load_w2([(w_in, 1024), (swa_w_sw, 1024)], 8, 128)
            proj_phase(H, "H", 8, wq, 8, epi_rope(QT, "QT", 0, 4, 0.125, False), pre_pf=rope_pre)
            wv_ = load_w(w_in, 8, 128, col0=1152)
            proj_phase(H, "H", 8, wk, 1, epi_rope(KT, "KT", 0, 6, 1.0, False), pre_pf=rope_pre)
            wo = load_w(swa_w_out[0], 8, 1024)
            vproj_phase(wv_[0], wv_[1], 128, None)
            attn_swa()
        wgu0 = None
        proj_phase(OT, "OT", 8, [wo], 8, epi_resid(g1), pre_pf=resid_pre(cur_x, cur_name), post=resid_post(XM, "XM", norm=(a2, b2, lambda tt: big(2 + tt % 2))))
        wgu = [load_w2([(w_gate[l], hf * 1408), (w_up[l], hf * 1408)], 8, 1408) for hf in range(1)]
        proj_phase(H, "H", 8, wgu[0], 11, epi_swiglu(0), post=swiglu_post(0))
        wgu1 = load_w2([(w_gate[l], 1408), (w_up[l], 1408)], 8, 1408)
        proj_phase(H, "H", 8, wgu1, 11, epi_swiglu(1), post=swiglu_post(1))
        wdn = load_w(w_down[l], 22, 1024)
        last = (l == nlayers - 1)
        dst, dname = (yT, "yT") if last else (xs[xs_names[l % 2]], xs_names[l % 2])
        proj_phase(A, "A", 22, [wdn], 8, epi_resid(g2), pre_pf=resid_pre(XM, "XM"), post=resid_post(dst, dname, outs if last else None,
                                    norm=None if last else (der(l + 1, 0), mod(l + 1, "sh1"), lambda tt: (AUX[:, :], AUXb))))
        cur_x, cur_name = dst, dname

    P.emit(final_waits=outs)
    st.close()
    return nc, P.stats


def _consts():
    p = np.arange(128)[:, None]
    c = np.arange(128)[None, :]
    cb = np.zeros((128, NCB), np.float32)
    cb[:, C_ID:C_ID + 128] = (p == c)
    cb[:, C_TRI:C_TRI + 128] = -((p >= c).astype(np.float32))
    cb[:, C_NONE:C_NONE + 128] = -1.0
    cb[:, C_ONE:C_ONE + 128] = 1.0
    cb[:, C_BLK:C_BLK + 128] = ((p // 64) == (c // 64))
    cb[:, C_MSB:C_MSB + 128] = NEG * (p >= c)
    c2 = np.arange(256)[None, :]
    cb[:, C_MMO:C_MMO + 256] = NEG * (p > c2)
    cb[:, C_MMO + 256:C_MMO + 512] = NEG * (128 + p > c2)
    cb[:, C_MSW:C_MSW + 128] = NEG * (c >= p)
    cb[:, C_MSW + 128:C_MSW + 256] = NEG * (p > c)
    onehot = (np.arange(16)[:, None] == (np.arange(S)[None, :] // 256)).astype(np.float32)
    tt = np.arange(32)[:, None]
    n = np.arange(16)[None, :]
    pm = np.where(n < tt // 2, 0.0, -1e30).astype(np.float32).reshape(1, 512)
    pm = np.ascontiguousarray(np.broadcast_to(pm, (128, 512)))
    inv_freq = (1.0 / (np.float32(10000.0) ** (np.arange(0, 64, 2, dtype=np.float32) / np.float32(64)))).astype(np.float32)
    ang = np.arange(S, dtype=np.float32)[:, None] * inv_freq[None, :]
    cos = np.cos(ang).astype(np.float32).T
    sin = np.sin(ang).astype(np.float32).T
    cosT = np.concatenate([cos, cos, cos, cos], 0)
    sinT = np.concatenate([-sin, sin, -sin, sin], 0)
    return (cb.astype(ml_dtypes.bfloat16), onehot.astype(ml_dtypes.bfloat16), pm,
            np.ascontiguousarray(cosT), np.ascontiguousarray(sinT))


def _swap_halves(w, nheads):
    k = w.shape[0]
    return np.ascontiguousarray(w.reshape(k, nheads, 2, 32)[:, :, ::-1, :].reshape(k, nheads * 64))


def make_in_maps(inputs):
    f = lambda a: np.ascontiguousarray(np.asarray(a, dtype=np.float32))
    x = f(inputs["x"])
    c = f(inputs["c"])
    cb, onehot, pm, cosT, sinT = _consts()
    ada_bT = f(f(inputs["ada_b"]).reshape(4, 48, 128).transpose(2, 0, 1).reshape(128, 192))
    gainT = f(f(inputs["norm_gain"]).reshape(4, 2, 8, 128).transpose(3, 0, 1, 2).reshape(128, 64))
    mg = f(inputs["moba_qk_gain"])[0]
    sg = f(inputs["swa_qk_gain"])[0]
    sw = lambda g: np.concatenate([g[32:], g[:32]])
    cols = [mg[0], sw(mg[0]), mg[1], sw(mg[1]), sg[0], sw(sg[0]), sg[1], sw(sg[1])]
    qkgT = f(np.stack([np.tile(v, 2) for v in cols], axis=1))
    sinkT = f(np.tile(f(inputs["swa_sinks"])[0][None, :], (128, 1)))
    moba_w_in = f(inputs["moba_w_in"])
    swa_w_in = f(inputs["swa_w_in"])
    shared = {
        "ada_w": f(inputs["ada_w"]), "ada_bT": ada_bT, "gainT": gainT,
        "ffn_w_gate": f(inputs["ffn_w_gate"]), "ffn_w_up": f(inputs["ffn_w_up"]), "ffn_w_down": f(inputs["ffn_w_down"]),
        "sb_w_in": f(inputs["sb_w_in"]), "sb_w_out": f(inputs["sb_w_out"]),
        "moba_w_in": moba_w_in, "moba_w_sw": _swap_halves(moba_w_in[0][:, :2048], 32), "moba_w_out": f(inputs["moba_w_out"]),
        "swa_w_in": swa_w_in, "swa_w_sw": _swap_halves(swa_w_in[0][:, :1152], 18), "swa_w_out": f(inputs["swa_w_out"]),
        "qkgT": qkgT, "sinkT": sinkT, "cosT": cosT, "sinT": sinT, "constb": cb, "onehot": onehot, "pastmask": pm,
    }
    maps = []
    for b in range(8):
        m = dict(shared)
        m["xT"] = np.ascontiguousarray(x[b].T)
        m["cT"] = np.ascontiguousarray(c[b].reshape(8, 128).T)
        maps.append(m)
    return maps


_CACHE = {}


def kernel(**inputs):
    if "nc" not in _CACHE:
        _CACHE["nc"] = build_program()[0]
    nc = _CACHE["nc"]
    maps = make_in_maps(inputs)
    res = run_bass_kernel_spmd(nc, maps, core_ids=list(range(8)))
    out = np.stack([np.ascontiguousarray(res.results[b]["yT"].T) for b in range(8)], axis=0)
    return out.astype(np.float32)
```

```python
import contextlib
import numpy as np
import ml_dtypes
import concourse.bass as bass
import concourse.mybir as mybir
from concourse.bass_utils import run_bass_kernel_spmd

F32 = mybir.dt.float32
BF16 = mybir.dt.bfloat16
AF = mybir.ActivationFunctionType
ALU = mybir.AluOpType
AX = mybir.AxisListType

S = 4096
D = 1024
DFF = 2816
NT = 8
EPS = 1e-6
NEG = -30000.0
EPOCH = 20000
DMA_SLOTS = {"sp": 16, "pool": 8, "act": 8}


class Buf:
    __slots__ = ("name", "writers", "readers", "excl")

    def __init__(self, name, excl=False):
        self.name = name
        self.excl = excl
        self.writers = {}
        self.readers = {}


class Op:
    __slots__ = ("stream", "fn", "deps", "signal", "val", "key", "is_dma", "prev_slot_op", "idx")


class Prog:
    STREAMS = ("pe", "act", "dve", "pool", "sp")

    def __init__(self, nc):
        self.nc = nc
        self.streams = {s: [] for s in self.STREAMS}
        self.ndma = {s: 0 for s in self.STREAMS}
        self.slot_last = {}

    def buf(self, name="b", excl=False):
        return Buf(name, excl)

    def _record(self, o, reads, writes, partial):
        deps = {}

        def add(d):
            for k, w in d.items():
                if w is o:
                    continue
                cur = deps.get(k)
                if cur is None or w.idx > cur.idx:
                    deps[k] = w

        for b in reads:
            add(b.writers)
            if b.excl:
                add(b.readers)
        for b in writes:
            add(b.readers)
            if not partial:
                add(b.writers)
        if o.stream == "pe" and not o.is_dma:
            deps = {k: w for k, w in deps.items() if not (w.stream == "pe" and not w.is_dma)}
        o.deps = list(deps.values())
        for b in reads:
            b.readers[o.key] = o
        for b in writes:
            if partial:
                b.writers[o.key] = o
            else:
                b.writers = {o.key: o}
        self.streams[o.stream].append(o)

    def op(self, stream, fn, reads=(), writes=(), partial=False):
        o = Op()
        o.stream = stream
        o.fn = fn
        o.signal = False
        o.val = None
        o.is_dma = False
        o.prev_slot_op = None
        o.idx = len(self.streams[stream])
        o.key = (stream, o.idx // EPOCH)
        self._record(o, reads, writes, partial)
        return o

    def dma(self, stream, out, in_, reads=(), writes=(), partial=False, **kw):
        o = Op()
        o.stream = stream
        o.fn = lambda e: e.dma_start(out=out, in_=in_, **kw)
        o.signal = True
        o.is_dma = True
        o.idx = len(self.streams[stream])
        i = self.ndma[stream]
        self.ndma[stream] += 1
        R = DMA_SLOTS[stream]
        o.key = ("dma", stream, i % R)
        o.val = 16 * (i // R + 1)
        o.prev_slot_op = self.slot_last.get(o.key)
        self.slot_last[o.key] = o
        self._record(o, reads, writes, partial)
        return o

    def emit(self, final_waits=()):
        nc = self.nc
        if final_waits:
            o = Op()
            o.stream = "sp"
            o.fn = None
            o.signal = False
            o.val = None
            o.is_dma = False
            o.prev_slot_op = None
            o.idx = len(self.streams["sp"])
            o.key = ("sp", o.idx // EPOCH)
            o.deps = list(final_waits)
            self.streams["sp"].append(o)
        for s in self.STREAMS:
            for o in self.streams[s]:
                for d in o.deps:
                    d.signal = True
        keys = {}
        for s in self.STREAMS:
            cnt = {}
            for o in self.streams[s]:
                if o.is_dma:
                    keys[o.key] = None
                    continue
                if o.signal:
                    c = cnt.get(o.key, 0) + 1
                    cnt[o.key] = c
                    o.val = c
                    keys[o.key] = None
        stack = contextlib.ExitStack()
        sems = {}
        for k in keys:
            sems[k] = stack.enter_context(nc.semaphore("s_" + "_".join(str(x) for x in k)))
        self.stats = {s: len(self.streams[s]) for s in self.STREAMS}
        self.stats["nsem"] = len(sems)

        def run(s, eng):
            waited = {}
            nw = 0
            for o in self.streams[s]:
                need = {}
                for d in o.deps:
                    if d.val > need.get(d.key, 0):
                        need[d.key] = d.val
                if o.is_dma and o.prev_slot_op is not None:
                    p = o.prev_slot_op
                    if p.val > need.get(p.key, 0):
                        need[p.key] = p.val
                for k, v in need.items():
                    if waited.get(k, 0) < v:
                        eng.wait_ge(sems[k], v)
                        waited[k] = v
                        nw += 1
                if o.fn is None:
                    continue
                ins = o.fn(eng)
                if o.is_dma:
                    ins.then_inc(sems[o.key], 16)
                elif o.signal:
                    ins.then_inc(sems[o.key], 1)
            self.stats["waits_" + s] = nw

        with nc.Block() as block:
            @block.tensor
            def _(e):
                run("pe", e)

            @block.scalar
            def _(e):
                run("act", e)

            @block.vector
            def _(e):
                run("dve", e)

            @block.gpsimd
            def _(e):
                run("pool", e)

            @block.sync
            def _(e):
                run("sp", e)
        stack.close()


C_ID, C_TRI, C_NONE, C_ONE, C_BLK, C_MSB, C_MMO, C_MSW, NCB = 0, 128, 256, 384, 512, 640, 768, 1280, 1536
KINDS = ("sb", "moba", "swa", "sb")
JIDX = (0, 0, 0, 1)


def build_program(nlayers=4, dump=()):
    nc = bass.Bass("TRN2", target_bir_lowering=False)
    P = Prog(nc)
    st = contextlib.ExitStack()

    def din(name, shape, dt=F32):
        return nc.dram_tensor(name, list(shape), dt, kind="ExternalInput").ap()

    def dscr(name, shape, dt):
        kind = "ExternalOutput" if name in dump else "Internal"
        return nc.dram_tensor(name, list(shape), dt, kind=kind).ap()

    xT = din("xT", [D, S])
    cT = din("cT", [128, 8])
    ada_w = din("ada_w", [4, D, 6 * D])
    ada_bT = din("ada_bT", [128, 192])
    gainT = din("gainT", [128, 64])
    w_gate = din("ffn_w_gate", [4, D, DFF])
    w_up = din("ffn_w_up", [4, D, DFF])
    w_down = din("ffn_w_down", [4, DFF, D])
    sb_w_in = din("sb_w_in", [2, D, 3 * D])
    sb_w_out = din("sb_w_out", [2, D, D])
    moba_w_in = din("moba_w_in", [1, D, 3 * D])
    moba_w_sw = din("moba_w_sw", [D, 2 * D])
    moba_w_out = din("moba_w_out", [1, D, D])
    swa_w_in = din("swa_w_in", [1, D, 1280])
    swa_w_sw = din("swa_w_sw", [D, 1152])
    swa_w_out = din("swa_w_out", [1, D, D])
    qkgT = din("qkgT", [128, 8])
    sinkT = din("sinkT", [128, 16])
    cosT = din("cosT", [128, S])
    sinT = din("sinT", [128, S])
    constb = din("constb", [128, NCB], BF16)
    onehot = din("onehot", [16, S], BF16)
    pastmask = din("pastmask", [128, 512])
    yT = nc.dram_tensor("yT", [D, S], F32, kind="ExternalOutput").ap()

    H = dscr("H", [D, S], BF16)
    QT = dscr("QT", [D, S], BF16)
    KT = dscr("KT", [D, S], BF16)
    V = dscr("V", [S, D], BF16)
    OT = dscr("OT", [D, S], BF16)
    U = dscr("U", [D, S], F32)
    DEN = dscr("DEN", [D, S], F32)
    XM = dscr("XM", [D, S], F32)
    XA = dscr("XA", [D, S], F32)
    XB = dscr("XB", [D, S], F32)
    A = dscr("A", [DFF, S], BF16)
    KMT = dscr("KMT", [D, 16], F32)

    dbufs = {}

    def Dd(name, tt):
        k = (name, tt)
        if k not in dbufs:
            dbufs[k] = P.buf(f"{name}{tt}")
        return dbufs[k]

    def Dall(name):
        return [Dd(name, tt) for tt in range(NT)]

    def sb(name, shape, dt):
        return st.enter_context(nc.sbuf_tensor(name, list(shape), dt))

    W = [sb("W0", [128, 22528], BF16), sb("W1", [128, 22528], BF16)]
    Wb = [P.buf("W0"), P.buf("W1")]
    BIG = sb("BIG", [128, 6 * 4096], BF16)
    BIGb = [P.buf(f"BIG{i}") for i in range(6)]
    F16 = [sb("F16a", [128, 4096], F32), sb("F16b", [128, 4096], F32)]
    F16b = [P.buf("F16a"), P.buf("F16b")]
    CB = sb("CB", [128, NCB], BF16)
    CBb = P.buf("CB")
    MOD = sb("MOD", [128, 4 * 48], F32)
    MODb = P.buf("MOD")
    DER = sb("DER", [128, 4 * 16], F32)
    DERb = P.buf("DER")
    GAIN = sb("GAIN", [128, 64], F32)
    ABT = sb("ABT", [128, 192], F32)
    QKG = sb("QKG", [128, 8], F32)
    ESINK = sb("ESINK", [128, 16], F32)
    CACT = sb("CACT", [128, 8], F32)
    MISCb = P.buf("MISC")
    NF = 5
    FT = [sb(f"FT{i}", [128, 512], F32) for i in range(NF)]
    FTb = [P.buf(f"FT{i}") for i in range(NF)]
    NB = 8
    BT = [sb(f"BT{i}", [128, 512], BF16) for i in range(NB)]
    BTb = [P.buf(f"BT{i}") for i in range(NB)]
    RT = [sb(f"RT{i}", [128, 512], BF16) for i in range(2)]
    RTb = [P.buf(f"RT{i}") for i in range(2)]
    AUX = sb("AUX", [128, 4096], BF16)
    VST = [AUX[:, i * 1024:(i + 1) * 1024] for i in range(2)]
    VSTb = [P.buf(f"VST{i}") for i in range(2)]
    CS0b = P.buf("CS0")
    CS = [AUX[:, 2048:4096].bitcast(F32), AUX[:, 0:2048].bitcast(F32)]
    CSb = [[CS0b], [VSTb[0], VSTb[1]]]
    AUXb = [VSTb[0], VSTb[1], CS0b]
    KM = sb("KM", [128, 128], F32)
    KMb = P.buf("KM")
    KMH = sb("KMH", [128, 32], F32)
    KMHb = P.buf("KMH")
    KMHB = sb("KMHB", [128, 16], BF16)
    KMHBb = P.buf("KMHB")
    PMK = sb("PMK", [128, 512], F32)
    PMKb = P.buf("PMK")
    GM = sb("GM", [128, 64], F32)
    GMb = P.buf("GM")
    TOP = sb("TOP", [128, 32], F32)
    TOPb = P.buf("TOP")
    SEL = sb("SEL", [128, 64], F32)
    SELb = P.buf("SEL")
    SPAD = sb("SPAD", [128, 512], BF16)
    SPADb = P.buf("SPAD")
    PSD = [st.enter_context(nc.psum_tensor(f"psd{i}", [128, 1024], F32)) for i in range(4)]
    PS = [PSD[i // 2][:, (i % 2) * 512:(i % 2 + 1) * 512] for i in range(8)]
    PSb = [P.buf(f"ps{i}", excl=True) for i in range(8)]

    rr = {"ps": 0, "ft": 0, "bt": 0, "sps": 0, "ops": 0}

    def bank():
        i = rr["ps"] % 8
        rr["ps"] += 1
        return PS[i], PSb[i]

    def sbank():
        i = rr["sps"] % 6
        rr["sps"] += 1
        return PS[i], PSb[i]

    def obank():
        i = 6 + rr["ops"] % 2
        rr["ops"] += 1
        return PS[i], PSb[i]

    def ft():
        i = rr["ft"] % NF
        rr["ft"] += 1
        return FT[i], FTb[i]

    def bt():
        i = rr["bt"] % NB
        rr["bt"] += 1
        return BT[i], BTb[i]

    def big(u0, n=1):
        return BIG[:, u0 * 4096:(u0 + n) * 4096], BIGb[u0:u0 + n]

    def fm(dram, tt):
        return dram.rearrange("(c p) t -> p c t", p=128)[:, :, tt * 512:(tt + 1) * 512]

    ident = CB[:, C_ID:C_ID + 128]
    negtri = CB[:, C_TRI:C_TRI + 128]
    negones = CB[:, C_NONE:C_NONE + 128]
    ones = CB[:, C_ONE:C_ONE + 128]
    blkones = CB[:, C_BLK:C_BLK + 128]
    mask_sb = CB[:, C_MSB:C_MSB + 128]
    mask_mo = CB[:, C_MMO:C_MMO + 512]
    mask_sw = CB[:, C_MSW:C_MSW + 256]

    P.dma("sp", CB[:], constb, writes=[CBb])
    P.dma("sp", CACT[:], cT, writes=[MISCb], partial=True)
    P.dma("sp", GAIN[:], gainT, writes=[MISCb], partial=True)
    P.dma("sp", ABT[:], ada_bT, writes=[MISCb], partial=True)
    P.dma("sp", QKG[:], qkgT, writes=[MISCb], partial=True)
    P.dma("sp", ESINK[:], sinkT, writes=[MISCb], partial=True)
    P.dma("sp", PMK[:], pastmask, writes=[PMKb])
    P.op("act", lambda e: e.activation(out=CACT[:], in_=CACT[:], func=AF.Silu), reads=[MISCb], writes=[MISCb])
    P.op("act", lambda e: e.activation(out=ESINK[:], in_=ESINK[:], func=AF.Exp), reads=[MISCb], writes=[MISCb])

    ONE = sb("ONE", [128, 1], F32)
    ONEb = P.buf("ONE")
    P.op("pool", lambda e: e.memset(ONE[:], 1.0), writes=[ONEb])

    def mod_view(r):
        return W[r][:, :].bitcast(F32)[:, 0:8 * 512].rearrange("p (k f) -> p k f", k=8)

    def mod_load(l, ct, r):
        wv = mod_view(r)
        src = ada_w[l].rearrange("(k p) f -> p k f", p=128)[:, :, ct * 512:(ct + 1) * 512]
        for k in range(8):
            P.dma("sp", wv[:, k, :], src[:, k, :], writes=[Wb[r]], partial=True)

    def mod_compute(l, ct, r, banks):
        (rps, rpb), (tps, tpb) = banks
        wv = mod_view(r)

        def mm(e):
            ins = None
            for k in range(8):
                ins = e.matmul(rps[0:1, :], lhsT=CACT[:, k:k + 1], rhs=wv[:, k, :], start=(k == 0), stop=(k == 7), skip_group_check=True)
            return ins
        P.op("pe", mm, reads=[Wb[r], MISCb], writes=[rpb])
        row, rowb = ft()
        P.op("dve", lambda e: e.tensor_copy(out=row[0:1, :], in_=rps[0:1, :]), reads=[rpb], writes=[rowb])

        def tr(e):
            ins = None
            for j in range(4):
                ins = e.matmul(tps[:, j:j + 1], lhsT=row[0:1, j * 128:(j + 1) * 128], rhs=ONE[0:1, 0:1],
                               start=True, stop=True, skip_group_check=True)
            return ins
        P.op("pe", tr, reads=[rowb, ONEb], writes=[tpb])
        c0 = l * 48 + ct * 4
        P.op("dve", lambda e: e.tensor_tensor(out=MOD[:, c0:c0 + 4], in0=tps[:, 0:4], in1=ABT[:, c0:c0 + 4], op=ALU.add),
             reads=[tpb, MISCb], writes=[MODb], partial=True)

    def mod_finish(l):
        for j in range(2):
            P.op("dve", lambda e, j=j: e.scalar_tensor_tensor(
                out=DER[:, l * 16 + j * 8:l * 16 + j * 8 + 8], in0=MOD[:, l * 48 + j * 24 + 8:l * 48 + j * 24 + 16],
                scalar=1.0, in1=GAIN[:, l * 16 + j * 8:l * 16 + j * 8 + 8], op0=ALU.add, op1=ALU.mult),
                reads=[MODb, MISCb], writes=[DERb], partial=True)

    for ct in range(12):
        mod_load(0, ct, ct % 2)
        mod_compute(0, ct, ct % 2, (sbank(), sbank()))
    mod_finish(0)
    bg_tasks = []

    def make_bg_tasks(r, getbanks):
        seq = [(l, ct) for l in range(1, nlayers) for ct in range(12)]
        tasks = []
        for k in range(len(seq) + 1):
            def task(k=k):
                if k >= 1:
                    l, ct = seq[k - 1]
                    mod_compute(l, ct, r, getbanks())
                    if ct == 11:
                        mod_finish(l)
                if k < len(seq):
                    mod_load(seq[k][0], seq[k][1], r)
            tasks.append(task)
        return tasks

    def mod(l, which):
        off = {"sh1": 0, "g1": 16, "sh2": 24, "g2": 40}[which]
        return MOD[:, l * 48 + off:l * 48 + off + 8]

    def der(l, j):
        return DER[:, l * 16 + j * 8:l * 16 + j * 8 + 8]

    wstate = {"i": 0}

    def load_w(dram2d, KC, ncols, col0=0, slot=None):
        r = wstate["i"] % 2 if slot is None else slot
        if slot is None:
            wstate["i"] += 1
        view = W[r][:, 0:KC * ncols].rearrange("p (k f) -> p k f", k=KC)
        for k in range(KC):
            c = 0
            while c < ncols:
                n = min(1024, ncols - c)
                P.dma("pool", view[:, k, c:c + n], dram2d[k * 128:(k + 1) * 128, col0 + c:col0 + c + n],
                      writes=[Wb[r]], partial=True)
                c += n
        return view, Wb[r]

    def load_w2(drams, KC, ncols_each):
        r = wstate["i"] % 2
        wstate["i"] += 1
        views = []
        base = 0
        for (dram2d, col0) in drams:
            view = W[r][:, base:base + KC * ncols_each].rearrange("p (k f) -> p k f", k=KC)
            base += KC * ncols_each
            for k in range(KC):
                c = 0
                while c < ncols_each:
                    n = min(1024, ncols_each - c)
                    P.dma("pool", view[:, k, c:c + n], dram2d[k * 128:(k + 1) * 128, col0 + c:col0 + c + n],
                          writes=[Wb[r]], partial=True)
                    c += n
            views.append((view, Wb[r]))
        return views

    def norm_tile(tt, a, b, h2, hb, defer=False):
        xt = F16[tt % 2][:, :].rearrange("p (c t) -> p c t", c=8)
        xb_ = F16b[tt % 2]
        h = h2.rearrange("p (c t) -> p c t", c=8)
        P.op("act", lambda e: e.activation(out=h, in_=xt, func=AF.Square), reads=[xb_], writes=hb)
        if defer:
            return lambda: norm_tile_b(tt, a, b, h2, hb)
        norm_tile_b(tt, a, b, h2, hb)

    def norm_tile_b(tt, a, b, h2, hb):
        xt = F16[tt % 2][:, :].rearrange("p (c t) -> p c t", c=8)
        xb_ = F16b[tt % 2]
        h = h2.rearrange("p (c t) -> p c t", c=8)
        ps, pb = bank()

        def mm(e):
            ins = None
            for c in range(8):
                ins = e.matmul(ps[:], lhsT=ones, rhs=h[:, c, :], start=(c == 0), stop=(c == 7))
            return ins
        P.op("pe", mm, reads=hb + [CBb], writes=[pb])
        r0, r0b = ft()
        P.op("act", lambda e: e.activation(out=r0[:], in_=ps[:], func=AF.Ln, scale=1.0 / D, bias=EPS),
             reads=[pb], writes=[r0b])
        P.op("act", lambda e: e.activation(out=r0[:], in_=r0[:], func=AF.Exp, scale=-0.5),
             reads=[r0b], writes=[r0b])
        for c in range(8):
            P.op("dve", lambda e, c=c: e.scalar_tensor_tensor(
                out=xt[:, c, :], in0=xt[:, c, :], scalar=a[:, c:c + 1], in1=r0[:], op0=ALU.mult, op1=ALU.mult),
                reads=[xb_, r0b, DERb], writes=[xb_], partial=True)
            P.op("act", lambda e, c=c: e.activation(
                out=h[:, c, :], in_=xt[:, c, :], func=AF.Identity, bias=b[:, c:c + 1], scale=1.0),
                reads=[xb_, MODb], writes=hb, partial=True)
        P.dma("sp", fm(H, tt), h, reads=hb, writes=[Dd("H", tt)])

    def norm_phase(src, sname, a, b):
        def ldx(tt):
            P.dma("sp", F16[tt % 2][:, :].rearrange("p (c t) -> p c t", c=8), fm(src, tt), reads=[Dd(sname, tt)], writes=[F16b[tt % 2]])
        ldx(0)
        for tt in range(NT):
            if tt + 1 < NT:
                ldx(tt + 1)
            h2, hb = big(tt % 2)
            norm_tile(tt, a, b, h2, hb)

    def proj_phase(src, sname, KC, wsets, n_out, epi, pre=None, post=None, pre_pf=None):
        def ld(tt):
            if KC == 8:
                r2, rb = big(tt % 2)
            else:
                r2, rb = big(3 * (tt % 2), 3)
                r2 = r2[:, 0:KC * 512]
            rhs = r2.rearrange("p (k t) -> p k t", k=KC)
            P.dma("sp", rhs, fm(src, tt), reads=[Dd(sname, tt)], writes=rb)
            if pre_pf:
                pre_pf(tt)
            return rhs, rb
        nxt = ld(0)
        postdef = [None]
        for tt in range(NT):
            rhs, rb = nxt
            loaded_next = False
            if pre:
                pre(tt)
            pending = None
            for o in range(n_out):
                if o == 1:
                    if postdef[0]:
                        postdef[0]()
                        postdef[0] = None
                    if tt + 1 < NT:
                        nxt = ld(tt + 1)
                        loaded_next = True
                banks = []
                for (Wv, Wbb) in wsets:
                    ps, pb = bank()

                    def mm(e, ps=ps, Wv=Wv, rhs=rhs, o=o):
                        ins = None
                        for k in range(KC):
                            ins = e.matmul(ps[:], lhsT=Wv[:, k, o * 128:(o + 1) * 128], rhs=rhs[:, k, :],
                                           start=(k == 0), stop=(k == KC - 1))
                        return ins
                    P.op("pe", mm, reads=rb + [Wbb], writes=[pb])
                    banks.append((ps, pb))
                if pending:
                    pending()
                pending = epi(tt, o, banks)
            if pending:
                pending()
            if postdef[0]:
                postdef[0]()
                postdef[0] = None
            if not loaded_next and tt + 1 < NT:
                nxt = ld(tt + 1)
            if post:
                postdef[0] = post(tt)
        if postdef[0]:
            postdef[0]()

    def epi_copy(dst, dname, row0, scale):
        def f(tt, o, banks):
            ps, pb = banks[0]
            t, tb = bt()
            if o % 2 == 0:
                P.op("dve", lambda e: e.tensor_scalar(out=t[:], in0=ps[:], scalar1=scale, scalar2=None, op0=ALU.mult),
                     reads=[pb], writes=[tb])
            else:
                P.op("act", lambda e: e.activation(out=t[:], in_=ps[:], func=AF.Copy, scale=scale), reads=[pb], writes=[tb])
            P.dma("sp", dst[row0 + o * 128:row0 + (o + 1) * 128, tt * 512:(tt + 1) * 512], t[:],
                  reads=[tb], writes=[Dd(dname, tt)], partial=True)
        return f

    def rope_pre(tt):
        cs = CS[tt % 2]
        P.dma("sp", cs[:, 0:512], cosT[:, tt * 512:(tt + 1) * 512], writes=CSb[tt % 2])
        P.dma("sp", cs[:, 512:1024], sinT[:, tt * 512:(tt + 1) * 512], writes=CSb[tt % 2], partial=True)

    def epi_rope(dst, dname, row0, gcol, qscale, kmean):
        lnb = float(np.log(qscale))

        def f(tt, o, banks):
            (q, qb), (qs, qsb) = banks
            cs = CS[tt % 2]
            csb = CSb[tt % 2]
            sq, sqb = bt()
            P.op("act", lambda e: e.activation(out=sq[:], in_=q[:], func=AF.Square), reads=[qb], writes=[sqb])
            t1, t1b = ft()
            t2, t2b = ft()
            P.op("dve", lambda e: e.scalar_tensor_tensor(out=t1[:], in0=q[:], scalar=QKG[:, gcol:gcol + 1], in1=cs[:, 0:512],
                                                         op0=ALU.mult, op1=ALU.mult), reads=[qb, MISCb] + csb, writes=[t1b])
            P.op("dve", lambda e: e.scalar_tensor_tensor(out=t2[:], in0=qs[:], scalar=QKG[:, gcol + 1:gcol + 2], in1=cs[:, 512:1024],
                                                         op0=ALU.mult, op1=ALU.mult), reads=[qsb, MISCb] + csb, writes=[t2b])
            P.op("pool", lambda e: e.tensor_tensor(out=t1[:], in0=t1[:], in1=t2[:], op=ALU.add), reads=[t1b, t2b], writes=[t1b])

            def deferred():
                ss, ssb = bank()
                P.op("pe", lambda e: e.matmul(ss[:], lhsT=blkones, rhs=sq[:], start=True, stop=True), reads=[sqb, CBb], writes=[ssb])
                rs, rsb = ft()
                P.op("act", lambda e: e.activation(out=rs[:], in_=ss[:], func=AF.Ln, scale=1.0 / 64, bias=EPS), reads=[ssb], writes=[rsb])
                if lnb == 0.0:
                    P.op("act", lambda e: e.activation(out=rs[:], in_=rs[:], func=AF.Exp, scale=-0.5), reads=[rsb], writes=[rsb])
                else:
                    P.op("act", lambda e: e.activation(out=rs[:], in_=rs[:], func=AF.Exp, scale=-0.5, bias=lnb), reads=[rsb], writes=[rsb])
                ob, obb = bt()
                P.op("pool", lambda e: e.tensor_tensor(out=ob[:], in0=t1[:], in1=rs[:], op=ALU.mult), reads=[t1b, rsb], writes=[obb])
                if kmean:
                    P.op("dve", lambda e: e.tensor_reduce(
                        out=KM[:, o * 16 + tt * 2:o * 16 + tt * 2 + 2], in_=ob[:, :].rearrange("p (n s) -> p n s", s=256),
                        axis=AX.X, op=ALU.add), reads=[obb], writes=[KMb], partial=True)
                P.dma("sp", dst[row0 + o * 128:row0 + (o + 1) * 128, tt * 512:(tt + 1) * 512], ob[:],
                      reads=[obb], writes=[Dd(dname, tt)], partial=True)
            return deferred
        return f

    def resid_pre(src, sname):
        def f(tt):
            xt = F16[tt % 2][:, :].rearrange("p (c t) -> p c t", c=8)
            P.dma("sp", xt, fm(src, tt), reads=[Dd(sname, tt)], writes=[F16b[tt % 2]])
        return f

    def epi_resid(g):
        def f(tt, o, banks):
            ps, pb = banks[0]
            xt = F16[tt % 2][:, :].rearrange("p (c t) -> p c t", c=8)
            P.op("dve", lambda e: e.scalar_tensor_tensor(out=xt[:, o, :], in0=ps[:], scalar=g[:, o:o + 1], in1=xt[:, o, :],
                                                         op0=ALU.mult, op1=ALU.add),
                 reads=[pb, F16b[tt % 2], MODb], writes=[F16b[tt % 2]], partial=True)
        return f

    def resid_post(dst, dname, outs=None, norm=None):
        def f(tt):
            xt = F16[tt % 2][:, :].rearrange("p (c t) -> p c t", c=8)
            o = P.dma("sp", fm(dst, tt), xt, reads=[F16b[tt % 2]], writes=[Dd(dname, tt)])
            if outs is not None:
                outs.append(o)
            if norm is not None:
                a, b, hsel = norm
                h2, hb = hsel(tt)
                return norm_tile(tt, a, b, h2, hb, defer=True)
        return f

    def epi_swiglu(half):
        def f(tt, o, banks):
            (g, gb), (u, ub) = banks
            a2, ab = big(2 + 2 * (tt % 2), 2)
            av = a2[:, 0:11 * 512].rearrange("p (c t) -> p c t", c=11)
            sg, sgb = ft()
            P.op("act", lambda e: e.activation(out=sg[:], in_=g[:], func=AF.Silu), reads=[gb], writes=[sgb])
            P.op("dve", lambda e: e.tensor_tensor(out=av[:, o, :], in0=sg[:], in1=u[:], op=ALU.mult),
                 reads=[sgb, ub], writes=ab, partial=True)
        return f

    def swiglu_post(half):
        def f(tt):
            a2, ab = big(2 + 2 * (tt % 2), 2)
            av = a2[:, 0:11 * 512].rearrange("p (c t) -> p c t", c=11)
            dst = A[half * 1408:(half + 1) * 1408, :].rearrange("(c p) t -> p c t", p=128)[:, :, tt * 512:(tt + 1) * 512]
            P.dma("sp", dst, av, reads=ab, writes=[Dd("A", tt)], partial=True)
        return f

    def vproj_phase(Wv, Wvb, ncols, vrows_cols):
        def ld(tt):
            r2, rb = big(tt % 2)
            rhs = r2.rearrange("p (k t) -> p k t", k=8)
            P.dma("sp", rhs, fm(H, tt), reads=[Dd("H", tt)], writes=rb)
            return rhs, rb
        nxt = ld(0)
        for tt in range(NT):
            rhs, rb = nxt
            if tt + 1 < NT:
                nxt = ld(tt + 1)
            for j in range(4):
                vs = VST[j % 2]
                vsb = VSTb[j % 2]
                nh = (ncols + 511) // 512
                for hf in range(nh):
                    n = min(512, ncols - hf * 512)
                    ps, pb = bank()

                    def mm(e, ps=ps, rhs=rhs, j=j, hf=hf, n=n):
                        ins = None
                        for k in range(8):
                            ins = e.matmul(ps[:, 0:n], lhsT=rhs[:, k, j * 128:(j + 1) * 128], rhs=Wv[:, k, hf * 512:hf * 512 + n],
                                           start=(k == 0), stop=(k == 7))
                        return ins
                    P.op("pe", mm, reads=rb + [Wvb], writes=[pb])
                    if hf % 2 == 0:
                        P.op("act", lambda e, ps=ps, vs=vs, hf=hf, n=n: e.activation(out=vs[:, hf * 512:hf * 512 + n], in_=ps[:, 0:n], func=AF.Copy),
                             reads=[pb], writes=[vsb], partial=True)
                    else:
                        P.op("dve", lambda e, ps=ps, vs=vs, hf=hf, n=n: e.tensor_copy(out=vs[:, hf * 512:hf * 512 + n], in_=ps[:, 0:n]),
                             reads=[pb], writes=[vsb], partial=True)
                P.dma("sp", V[tt * 512 + j * 128:tt * 512 + (j + 1) * 128, 0:ncols], vs[:, 0:ncols],
                      reads=[vsb], writes=[Dd("V", tt)], partial=True)

    def attn_sb(bg=None):
        SUB = []
        for x in range(2):
            P.op("pool", lambda e, x=x: e.memset(F16[x][:, 0:2], 0.0), writes=[F16b[x]])
            fb = F16[x][:, :].bitcast(BF16)
            for i in range(8):
                SUB.append((fb[:, i * 1024:(i + 1) * 1024].rearrange("p (h n) -> p h n", h=2), P.buf(f"sub{x}{i}"), F16b[x]))
        E2, SP2, A2, R2 = SUB[0:3], SUB[3:7], SUB[7:10], [SUB[10:12], SUB[12:14]]
        rot = {"e": 0, "sp": 0, "a": 0, "s": 0}

        def nxt(lst, key):
            t = lst[rot[key] % len(lst)]
            rot[key] += 1
            return t

        def sdb():
            k = rot["s"] % 3
            rot["s"] += 1
            return PSD[k][:, :].rearrange("p (h n) -> p h n", h=2), [PSb[2 * k], PSb[2 * k + 1]]

        def bg_banks():
            k = rot["s"] % 3
            rot["s"] += 1
            return (PS[2 * k], PSb[2 * k]), (PS[2 * k + 1], PSb[2 * k + 1])
        if bg is not None and not bg:
            bg.extend(make_bg_tasks(wstate["i"] % 2, bg_banks))

        def load_pair(j):
            b = j % 2
            kt, ktb = big(3 * b)
            qt, qtb = big(3 * b + 1)
            vt, vtb = big(3 * b + 2)
            P.dma("sp", kt, KT[j * 128:(j + 1) * 128, :], reads=Dall("KT"), writes=ktb)
            P.dma("sp", qt, QT[j * 128:(j + 1) * 128, :], reads=Dall("QT"), writes=qtb)
            vv = vt.rearrange("p (n f) -> p n f", f=128)
            vsrc = V.rearrange("(n p) f -> p n f", p=128)[:, :, j * 128:(j + 1) * 128]
            for q4 in range(4):
                P.dma("sp", vv[:, q4 * 8:(q4 + 1) * 8, :], vsrc[:, q4 * 8:(q4 + 1) * 8, :], reads=Dall("V"), writes=vtb, partial=True)

        units = []
        for j in range(8):
            for i in range(8):
                for r, kb in enumerate(range(4 * i + 3, -1, -1)):
                    units.append(dict(j=j, i=i, kb=kb, r=r, c0=max(0, kb - 4 * i) * 128, diag=(kb >= 4 * i),
                                      first=(r == 0), last=(kb == 0), newpair=(i == 0 and r == 0)))
        load_pair(0)
        cur = {}

        def s1(u):
            j, i, kb, c0 = u["j"], u["i"], u["kb"], u["c0"]
            b = j % 2
            if u["first"]:
                cur["ob"] = obank()
                for (r2, r2b, r2g) in R2[i % 2]:
                    P.op("pool", lambda e, r2=r2: e.memset(r2, 0.0), reads=[r2g], writes=[r2b])
            u["ob"] = cur["ob"]
            kt, ktb = big(3 * b)
            qt, qtb = big(3 * b + 1)
            S3v, sb_ = sdb()
            u["S"], u["Sb"] = S3v, sb_

            def mm(e):
                ins = None
                for hh in (0, 1):
                    ins = e.matmul(S3v[:, hh, c0:512], lhsT=kt[hh * 64:(hh + 1) * 64, kb * 128:(kb + 1) * 128],
                                   rhs=qt[hh * 64:(hh + 1) * 64, i * 512 + c0:(i + 1) * 512], start=True, stop=False,
                                   skip_group_check=True)
                if u["diag"]:
                    for hh in (0, 1):
                        ins = e.matmul(S3v[:, hh, c0:c0 + 128], lhsT=ident, rhs=mask_sb, start=False, stop=False, skip_group_check=True)
                return ins
            P.op("pe", mm, reads=ktb + qtb + [CBb], writes=sb_)
            et, etb, etg = nxt(E2, "e")
            spt, sptb, sptg = nxt(SP2, "sp")
            u["sp"], u["spb"], u["spg"] = spt, sptb, sptg
            P.op("act", lambda e: e.activation(out=et[:, :, c0:512], in_=S3v[:, :, c0:512], func=AF.Exp), reads=sb_ + [etg], writes=[etb])
            P.op("act", lambda e: e.activation(out=spt[:, :, c0:512], in_=et[:, :, c0:512], func=AF.Ln, bias=1.0),
                 reads=[etb, sptg], writes=[sptb])

        def s3(u):
            i, c0 = u["i"], u["c0"]
            S3v, sb_, spt, sptb, sptg = u["S"], u["Sb"], u["sp"], u["spb"], u["spg"]
            r2, r2b, r2g = R2[i % 2][u["r"] % 2]
            rn, rnb, rng = R2[i % 2][(u["r"] + 1) % 2]

            def mm(e):
                ins = None
                for hh in (0, 1):
                    ins = e.matmul(S3v[:, hh, c0:512], lhsT=negtri, rhs=spt[:, hh, c0:512], start=False, stop=u["first"], skip_group_check=True)
                if not u["first"]:
                    for hh in (0, 1):
                        ins = e.matmul(S3v[:, hh, c0:512], lhsT=negones, rhs=r2[:, hh, c0:512], start=False, stop=True, skip_group_check=True)
                return ins
            P.op("pe", mm, reads=[sptb, CBb, r2b], writes=sb_)
            at, atb, atg = nxt(A2, "a")
            u["a"], u["ab"] = at, atb
            P.op("act", lambda e: e.activation(out=at[:, :, c0:512], in_=S3v[:, :, c0:512], func=AF.Exp), reads=sb_ + [atg], writes=[atb])
            if not u["last"]:
                P.op("dve", lambda e: e.tensor_tensor(out=rn[:, :, c0:512], in0=r2[:, :, c0:512], in1=spt[:, :, c0:512], op=ALU.add),
                     reads=[sptb, r2b, rng], writes=[rnb])

        def s5(u):
            j, i, kb, c0 = u["j"], u["i"], u["kb"], u["c0"]
            b = j % 2
            vt, vtb = big(3 * b + 2)
            vv = vt.rearrange("p (n f) -> p n f", f=128)
            ops_, opb = u["ob"]
            at, atb = u["a"], u["ab"]

            def mm(e):
                ins = None
                for hh in (0, 1):
                    ins = e.matmul(ops_[hh * 64:(hh + 1) * 64, c0:512], lhsT=vv[:, kb, hh * 64:(hh + 1) * 64], rhs=at[:, hh, c0:512],
                                   start=u["first"], stop=u["last"], skip_group_check=True)
                return ins
            P.op("pe", mm, reads=vtb + [atb], writes=[opb], partial=True)
            if u["last"]:
                t, tb = bt()
                if i % 2 == 0:
                    P.op("act", lambda e: e.activation(out=t[:], in_=ops_[:], func=AF.Copy), reads=[opb], writes=[tb])
                else:
                    P.op("dve", lambda e: e.tensor_copy(out=t[:], in_=ops_[:]), reads=[opb], writes=[tb])
                P.dma("sp", OT[j * 128:(j + 1) * 128, i * 512:(i + 1) * 512], t[:], reads=[tb], writes=[Dd("OT", i)], partial=True)

        N = len(units)
        for idx in range(N + 2):
            if idx < N:
                s1(units[idx])
            if 0 <= idx - 1 < N:
                s3(units[idx - 1])
            if 0 <= idx - 2 < N:
                s5(units[idx - 2])
            if 0 <= idx - 1 < N and units[idx - 1]["newpair"] and units[idx - 1]["j"] + 1 < 8:
                load_pair(units[idx - 1]["j"] + 1)
            if bg and idx % 28 == 10:
                bg.pop(0)()
        while bg:
            bg.pop(0)()

    def attn_moba():
        for b in range(2):
            vt, vtb = big(3 * b + 2)
            vv = vt.rearrange("p (n f) -> p n f", f=128)
            P.op("pool", lambda e, vv=vv: e.memset(vv[:, :, 64:128], 1.0), writes=vtb)
        P.op("pool", lambda e: e.memset(SPAD[:], 0.0), writes=[SPADb])
        P.dma("sp", KMT.rearrange("(c p) n -> p c n", p=128), KM[:, :].rearrange("p (c n) -> p c n", c=8), reads=[KMb], writes=[Dd("KMT", 0)])

        def load_head(h):
            b = h % 2
            ka, kab = big(3 * b)
            qa, qab = big(3 * b + 1)
            vt, vtb = big(3 * b + 2)
            vv = vt.rearrange("p (n f) -> p n f", f=128)
            P.dma("sp", ka[0:64, :], KT[h * 64:(h + 1) * 64, :], reads=Dall("KT"), writes=kab)
            P.dma("sp", ka[64:80, :], onehot, writes=kab, partial=True)
            P.dma("sp", qa[0:64, :], QT[h * 64:(h + 1) * 64, :], reads=Dall("QT"), writes=qab)
            vsrc = V.rearrange("(n p) f -> p n f", p=128)[:, :, h * 64:(h + 1) * 64]
            for q4 in range(4):
                P.dma("sp", vv[:, q4 * 8:(q4 + 1) * 8, 0:64], vsrc[:, q4 * 8:(q4 + 1) * 8, :], reads=Dall("V"), writes=vtb, partial=True)

        def pp_init(h):
            kmh = KMH[0:64, (h % 2) * 16:(h % 2) * 16 + 16]
            P.dma("sp", kmh, KMT[h * 64:(h + 1) * 64, :], reads=[Dd("KMT", 0)], writes=[KMHb])
            P.op("dve", lambda e: e.tensor_copy(out=KMHB[0:64, :], in_=kmh), reads=[KMHb], writes=[KMHBb])

        def pp_a(h, g):
            b = h % 2
            qa, qab = big(3 * b + 1)
            gps, gpb = sbank()

            def mm(e):
                ins = None
                for tq in range(4):
                    t0 = (g * 4 + tq) * 128
                    ins = e.matmul(gps[:, tq * 16:(tq + 1) * 16], lhsT=qa[0:64, t0:t0 + 128], rhs=KMHB[0:64, :], start=True, stop=True,
                                   skip_group_check=True)
                return ins
            P.op("pe", mm, reads=qab + [KMHBb], writes=[gpb])
            P.op("dve", lambda e: e.tensor_tensor(out=GM[:], in0=gps[:, 0:64], in1=PMK[:, g * 64:(g + 1) * 64], op=ALU.add),
                 reads=[gpb, PMKb], writes=[GMb])
            for tq in range(4):
                P.op("dve", lambda e, tq=tq: e.max(out=TOP[:, tq * 8:(tq + 1) * 8], in_=GM[:, tq * 16:(tq + 1) * 16]),
                     reads=[GMb], writes=[TOPb], partial=True)
            gm3 = GM[:, :].rearrange("p (q n) -> p q n", n=16)
            top3 = TOP[:, :].rearrange("p (q n) -> p q n", n=8)
            sel3 = SEL[:, :].rearrange("p (q n) -> p q n", n=16)
            P.op("dve", lambda e: e.tensor_tensor(out=sel3, in0=gm3, in1=top3[:, :, 2:3].to_broadcast([128, 4, 16]), op=ALU.is_lt),
                 reads=[GMb, TOPb], writes=[SELb])
            sp3 = SPAD[:, :].rearrange("p (q n) -> p q n", n=128)
            P.op("dve", lambda e: e.tensor_scalar(out=sp3[:, :, 64:80], in0=sel3, scalar1=NEG, scalar2=None, op0=ALU.mult),
                 reads=[SELb], writes=[SPADb])

        def pp_b(h, g):
            b = h % 2
            qa, qab = big(3 * b + 1)
            sp3 = SPAD[:, :].rearrange("p (q n) -> p q n", n=128)
            tps, tpb = sbank()

            def mm2(e):
                ins = None
                for tq in range(4):
                    ins = e.matmul(tps[:, tq * 128:(tq + 1) * 128], lhsT=sp3[:, tq, :], rhs=ident, start=True, stop=True, skip_group_check=True)
                return ins
            P.op("pe", mm2, reads=[SPADb, CBb], writes=[tpb])
            P.op("act", lambda e: e.activation(out=qa[64:80, g * 512:(g + 1) * 512], in_=tps[64:80, :], func=AF.Copy),
                 reads=[tpb], writes=qab, partial=True)

        def prepass(h):
            pp_init(h)
            for g in range(8):
                pp_a(h, g)
                pp_b(h, g)

        units = []
        for h in range(16):
            for qi in range(8):
                for kb in range(4 * qi + 4):
                    units.append(dict(h=h, qi=qi, kb=kb, r=kb - 4 * qi, first=(kb == 0), last=(kb == 4 * qi + 3),
                                      newhead=(qi == 0 and kb == 0), k=len(units) % 144))
        load_head(0)
        prepass(0)
        cur = {}

        def s1(u):
            h, qi, kb, r = u["h"], u["qi"], u["kb"], u["r"]
            b = h % 2
            if h + 1 < 16:
                k = u["k"]
                if k >= 30 and (k - 30) % 6 == 0 and (k - 30) // 6 <= 8:
                    g = (k - 30) // 6
                    if g == 0:
                        pp_init(h + 1)
                    if g >= 1:
                        pp_b(h + 1, g - 1)
                    if g <= 7:
                        pp_a(h + 1, g)
            if u["first"]:
                cur["ob"] = obank()
            u["ob"] = cur["ob"]
            ka, kab = big(3 * b)
            qa, qab = big(3 * b + 1)
            ps, pb = sbank()
            u["ps"], u["pb"] = ps, pb
            q0 = qi * 512
            kc = slice(kb * 128, (kb + 1) * 128)
            if r < 0:
                c0 = 0
                P.op("pe", lambda e: e.matmul(ps[:, 0:512], lhsT=ka[0:80, kc], rhs=qa[0:80, q0:q0 + 512],
                                              start=True, stop=True, skip_group_check=True), reads=kab + qab, writes=[pb])
            elif r < 2:
                c0 = 0

                def mm(e):
                    e.matmul(ps[:, 0:256], lhsT=ka[0:64, kc], rhs=qa[0:64, q0:q0 + 256], start=True, stop=False, skip_group_check=True)
                    e.matmul(ps[:, 0:256], lhsT=ident, rhs=mask_mo[:, r * 256:(r + 1) * 256], start=False, stop=False, skip_group_check=True)
                    return e.matmul(ps[:, 256:512], lhsT=ka[0:80, kc], rhs=qa[0:80, q0 + 256:q0 + 512], start=False, stop=True,
                                    skip_group_check=True)
                P.op("pe", mm, reads=kab + qab + [CBb], writes=[pb])
            else:
                c0 = 256

                def mm(e):
                    e.matmul(ps[:, 256:512], lhsT=ka[0:64, kc], rhs=qa[0:64, q0 + 256:q0 + 512], start=True, stop=False, skip_group_check=True)
                    return e.matmul(ps[:, 256:512], lhsT=ident, rhs=mask_mo[:, (r - 2) * 256:(r - 1) * 256], start=False, stop=True,
                                    skip_group_check=True)
                P.op("pe", mm, reads=kab + qab + [CBb], writes=[pb])
            u["c0"] = c0
            pt, ptb = bt()
            u["p"], u["pbuf"] = pt, ptb
            P.op("act", lambda e: e.activation(out=pt[:, c0:512], in_=ps[:, c0:512], func=AF.Exp), reads=[pb], writes=[ptb])

        def s3(u):
            h, qi, kb, c0 = u["h"], u["qi"], u["kb"], u["c0"]
            b = h % 2
            vt, vtb = big(3 * b + 2)
            vv = vt.rearrange("p (n f) -> p n f", f=128)
            ops_, opb = u["ob"]
            pt, ptb = u["p"], u["pbuf"]
            P.op("pe", lambda e: e.matmul(ops_[:, c0:512], lhsT=vv[:, kb, :], rhs=pt[:, c0:512], start=u["first"], stop=u["last"],
                                          skip_group_check=True), reads=vtb + [ptb], writes=[opb], partial=True)
            if u["last"]:
                t, tb = ft()
                P.op("act", lambda e: e.activation(out=t[0:64, :], in_=ops_[64:128, :], func=AF.Ln), reads=[opb], writes=[tb])
                P.op("act", lambda e: e.activation(out=t[0:64, :], in_=t[0:64, :], func=AF.Exp, scale=-1.0), reads=[tb], writes=[tb])
                ot, otb = bt()
                P.op("dve", lambda e: e.tensor_tensor(out=ot[0:64, :], in0=ops_[0:64, :], in1=t[0:64, :], op=ALU.mult),
                     reads=[opb, tb], writes=[otb])
                P.dma("sp", OT[h * 64:(h + 1) * 64, qi * 512:(qi + 1) * 512], ot[0:64, :], reads=[otb], writes=[Dd("OT", qi)], partial=True)

        N = len(units)
        SK = 4
        for idx in range(N + SK):
            if idx < N:
                s1(units[idx])
            if 0 <= idx - SK < N:
                s3(units[idx - SK])
            k = idx - SK + 1
            if 0 <= k < N and units[k]["newhead"] and units[k]["h"] + 1 < 16:
                load_head(units[k]["h"] + 1)

    def attn_swa():
        for b in range(2):
            vt, vtb = big(3 * b + 2)
            vv = vt.rearrange("p (n f) -> p n f", f=128)
            P.op("pool", lambda e, vv=vv: e.memset(vv[:, :, 64:128], 1.0), writes=vtb)

        def load_pair(j):
            b = j % 2
            g = j // 4
            qp, qpb = big(3 * b)
            k2, k2b = big(3 * b + 1)
            vt, vtb = big(3 * b + 2)
            vv = vt.rearrange("p (n f) -> p n f", f=128)
            P.dma("sp", qp, QT[j * 128:(j + 1) * 128, :], reads=Dall("QT"), writes=qpb)
            P.dma("sp", k2[0:64, :], KT[g * 64:(g + 1) * 64, :], reads=Dall("KT"), writes=k2b)
            P.dma("sp", k2[64:128, :], KT[g * 64:(g + 1) * 64, :], reads=Dall("KT"), writes=k2b, partial=True)
            vsrc = V.rearrange("(n p) f -> p n f", p=128)[:, :, g * 64:(g + 1) * 64]
            for q4 in range(4):
                P.dma("sp", vv[:, q4 * 8:(q4 + 1) * 8, 0:64], vsrc[:, q4 * 8:(q4 + 1) * 8, :], reads=Dall("V"), writes=vtb, partial=True)

        units = []
        for j in range(8):
            for hh in (0, 1):
                for n in range(32):
                    units.append(dict(j=j, hh=hh, n=n, newpair=(hh == 0 and n == 0)))
        load_pair(0)
        cur = {}

        def s1(u):
            j, hh, n = u["j"], u["hh"], u["n"]
            b = j % 2
            if n % 4 == 0:
                cur["ob"] = obank()
            u["ob"] = cur["ob"]
            qp, qpb = big(3 * b)
            k2, k2b = big(3 * b + 1)
            ps, pb = sbank()
            u["ps"], u["pb"] = ps, pb
            rows = slice(hh * 64, (hh + 1) * 64)
            c0 = 128 if n == 0 else 0
            u["c0"] = c0

            def mm(e):
                if n > 0:
                    e.matmul(ps[:, 0:128], lhsT=k2[rows, (n - 1) * 128:n * 128], rhs=qp[rows, n * 128:(n + 1) * 128], start=True, stop=False,
                             skip_group_check=True)
                e.matmul(ps[:, 128:256], lhsT=k2[rows, n * 128:(n + 1) * 128], rhs=qp[rows, n * 128:(n + 1) * 128], start=(n == 0), stop=False,
                         skip_group_check=True)
                return e.matmul(ps[:, c0:256], lhsT=ident, rhs=mask_sw[:, c0:256], start=False, stop=True, skip_group_check=True)
            P.op("pe", mm, reads=qpb + k2b + [CBb], writes=[pb])
            pt, ptb = bt()
            u["p"], u["pbuf"] = pt, ptb
            P.op("act", lambda e: e.activation(out=pt[:, c0:256], in_=ps[:, c0:256], func=AF.Exp), reads=[pb], writes=[ptb])

        def s3(u):
            j, hh, n, c0 = u["j"], u["hh"], u["n"], u["c0"]
            b = j % 2
            h = 2 * j + hh
            vt, vtb = big(3 * b + 2)
            vv = vt.rearrange("p (n f) -> p n f", f=128)
            ops_, opb = u["ob"]
            pt, ptb = u["p"], u["pbuf"]
            cc = (n % 4) * 128

            def mm(e):
                if n > 0:
                    e.matmul(ops_[:, cc:cc + 128], lhsT=vv[:, n - 1, :], rhs=pt[:, 0:128], start=True, stop=False, skip_group_check=True)
                return e.matmul(ops_[:, cc:cc + 128], lhsT=vv[:, n, :], rhs=pt[:, 128:256], start=(n == 0), stop=True, skip_group_check=True)
            P.op("pe", mm, reads=vtb + [ptb], writes=[opb], partial=True)
            if n % 4 == 3:
                t, tb = ft()
                P.op("act", lambda e: e.activation(out=t[0:64, :], in_=ops_[64:128, :], func=AF.Ln, bias=ESINK[64:128, h:h + 1], scale=1.0),
                     reads=[opb, MISCb], writes=[tb])
                P.op("act", lambda e: e.activation(out=t[0:64, :], in_=t[0:64, :], func=AF.Exp, scale=-1.0), reads=[tb], writes=[tb])
                ot, otb = bt()
                P.op("dve", lambda e: e.tensor_tensor(out=ot[0:64, :], in0=ops_[0:64, :], in1=t[0:64, :], op=ALU.mult),
                     reads=[opb, tb], writes=[otb])
                c = (n - 3) * 128
                P.dma("sp", OT[h * 64:(h + 1) * 64, c:c + 512], ot[0:64, :], reads=[otb], writes=[Dd("OT", c // 512)], partial=True)

        N = len(units)
        SK = 4
        for idx in range(N + SK):
            if idx < N:
                s1(units[idx])
            if 0 <= idx - SK < N:
                s3(units[idx - SK])
            k = idx - SK + 1
            if 0 <= k < N and units[k]["newpair"] and units[k]["j"] + 1 < 8:
                load_pair(units[k]["j"] + 1)

    def normalize_phase():
        it2 = 0
        for tt in range(NT):
            for hf in range(2):
                cols = slice(tt * 512 + hf * 256, tt * 512 + (hf + 1) * 256)
                uu = F16[0][:, hf * 2048:(hf + 1) * 2048].rearrange("p (c t) -> p c t", c=8)
                dd = F16[1][:, hf * 2048:(hf + 1) * 2048].rearrange("p (c t) -> p c t", c=8)
                usrc = U.rearrange("(c p) t -> p c t", p=128)[:, :, cols]
                dsrc = DEN.rearrange("(c p) t -> p c t", p=128)[:, :, cols]
                P.dma("sp", uu, usrc, reads=[Dd("U", tt)], writes=[F16b[0]], partial=True)
                P.dma("sp", dd, dsrc, reads=[Dd("DEN", tt)], writes=[F16b[1]], partial=True)
                P.op("dve", lambda e, dd=dd: e.reciprocal(out=dd, in_=dd), reads=[F16b[1]], writes=[F16b[1]])
                o2, ob = big(it2 % 2)
                it2 += 1
                ov = o2[:, 0:2048].rearrange("p (c t) -> p c t", c=8)
                P.op("pool", lambda e, uu=uu, dd=dd, ov=ov: e.tensor_tensor(out=ov, in0=uu, in1=dd, op=ALU.mult),
                     reads=[F16b[0], F16b[1]], writes=ob)
                P.dma("sp", OT.rearrange("(c p) t -> p c t", p=128)[:, :, cols], ov, reads=ob, writes=[Dd("OT", tt)], partial=True)

    outs = []
    xs_names = ["XA", "XB"]
    xs = {"XA": XA, "XB": XB}
    cur_x, cur_name = xT, "xT"
    for l in range(nlayers):
        kind = KINDS[l]
        jl = JIDX[l]
        a1, b1, g1 = der(l, 0), mod(l, "sh1"), mod(l, "g1")
        a2, b2, g2 = der(l, 1), mod(l, "sh2"), mod(l, "g2")
        if l == 0:
            norm_phase(cur_x, cur_name, a1, b1)
        if kind == "sb":
            w_in = sb_w_in[jl]
            wq = load_w(w_in, 8, 2048)
            wv_ = load_w(w_in, 8, 1024, col0=2048)

            def epi_qk(tt, o, banks, fq=epi_copy(QT, "QT", 0, 0.125), fk=epi_copy(KT, "KT", 0, 1.0)):
                if o < 8:
                    fq(tt, o, banks)
                else:
                    fk(tt, o - 8, banks)
            proj_phase(H, "H", 8, [wq], 16, epi_qk)
            wo = load_w(sb_w_out[jl], 8, 1024)
            vproj_phase(wv_[0], wv_[1], 1024, None)
            attn_sb(bg=bg_tasks if (l == 0 and nlayers > 1) else None)
        elif kind == "moba":
            w_in = moba_w_in[0]
            wq = load_w2([(w_in, 0), (moba_w_sw, 0)], 8, 1024)
            wk = load_w2([(w_in, 1024), (moba_w_sw, 1024)], 8, 1024)
            proj_phase(H, "H", 8, wq, 8, epi_rope(QT, "QT", 0, 0, 0.125, False), pre_pf=rope_pre)
            wv_ = load_w(w_in, 8, 1024, col0=2048)
            proj_phase(H, "H", 8, wk, 8, epi_rope(KT, "KT", 0, 2, 1.0, True), pre_pf=rope_pre)
            wo = load_w(moba_w_out[0], 8, 1024)
            vproj_phase(wv_[0], wv_[1], 1024, None)
            attn_moba()
        else:
            w_in = swa_w_in[0]
            wq = load_w2([(w_in, 0), (swa_w_sw, 0)], 8, 1024)
            wk = load_w2([(w_in, 1024), (swa_w_sw, 1024)], 8, 128)
            proj_phase(H, "H", 8, wq, 8, epi_rope(QT, "QT", 0, 4, 0.125, False), pre_pf=rope_pre)
            wv_ = load_w(w_in, 8, 128, col0=1152)
            proj_phase(H, "H", 8, wk, 1, epi_rope(KT, "KT", 0, 6, 1.0, False), pre_pf=rope_pre)
            wo = load_w(swa_w_out[0], 8, 1024)
            vproj_phase(wv_[0], wv_[1], 128, None)
            attn_swa()
        wgu0 = None
        proj_phase(OT, "OT", 8, [wo], 8, epi_resid(g1), pre_pf=resid_pre(cur_x, cur_name), post=resid_post(XM, "XM", norm=(a2, b2, lambda tt: big(2 + tt % 2))))
        wgu = [load_w2([(w_gate[l], hf * 1408), (w_up[l], hf * 1408)], 8, 1408) for hf in range(1)]
        proj_phase(H, "H", 8, wgu[0], 11, epi_swiglu(0), post=swiglu_post(0))
        wgu1 = load_w2([(w_gate[l], 1408), (w_up[l], 1408)], 8, 1408)
        proj_phase(H, "H", 8, wgu1, 11, epi_swiglu(1), post=swiglu_post(1))
        wdn = load_w(w_down[l], 22, 1024)
        last = (l == nlayers - 1)
        dst, dname = (yT, "yT") if last else (xs[xs_names[l % 2]], xs_names[l % 2])
        proj_phase(A, "A", 22, [wdn], 8, epi_resid(g2), pre_pf=resid_pre(XM, "XM"), post=resid_post(dst, dname, outs if last else None,
                                    norm=None if last else (der(l + 1, 0), mod(l + 1, "sh1"), lambda tt: (AUX[:, :], AUXb))))
        cur_x, cur_name = dst, dname

    P.emit(final_waits=outs)
    st.close()
    return nc, P.stats


def _consts():
    p = np.arange(128)[:, None]
    c = np.arange(128)[None, :]
    cb = np.zeros((128, NCB), np.float32)
    cb[:, C_ID:C_ID + 128] = (p == c)
    cb[:, C_TRI:C_TRI + 128] = -((p >= c).astype(np.float32))
    cb[:, C_NONE:C_NONE + 128] = -1.0
    cb[:, C_ONE:C_ONE + 128] = 1.0
    cb[:, C_BLK:C_BLK + 128] = ((p // 64) == (c // 64))
    cb[:, C_MSB:C_MSB + 128] = NEG * (p >= c)
    c2 = np.arange(256)[None, :]
    cb[:, C_MMO:C_MMO + 256] = NEG * (p > c2)
    cb[:, C_MMO + 256:C_MMO + 512] = NEG * (128 + p > c2)
    cb[:, C_MSW:C_MSW + 128] = NEG * (c >= p)
    cb[:, C_MSW + 128:C_MSW + 256] = NEG * (p > c)
    onehot = (np.arange(16)[:, None] == (np.arange(S)[None, :] // 256)).astype(np.float32)
    tt = np.arange(32)[:, None]
    n = np.arange(16)[None, :]
    pm = np.where(n < tt // 2, 0.0, -1e30).astype(np.float32).reshape(1, 512)
    pm = np.ascontiguousarray(np.broadcast_to(pm, (128, 512)))
    inv_freq = (1.0 / (np.float32(10000.0) ** (np.arange(0, 64, 2, dtype=np.float32) / np.float32(64)))).astype(np.float32)
    ang = np.arange(S, dtype=np.float32)[:, None] * inv_freq[None, :]
    cos = np.cos(ang).astype(np.float32).T
    sin = np.sin(ang).astype(np.float32).T
    cosT = np.concatenate([cos, cos, cos, cos], 0)
    sinT = np.concatenate([-sin, sin, -sin, sin], 0)
    return (cb.astype(ml_dtypes.bfloat16), onehot.astype(ml_dtypes.bfloat16), pm,
            np.ascontiguousarray(cosT), np.ascontiguousarray(sinT))


def _swap_halves(w, nheads):
    k = w.shape[0]
    return np.ascontiguousarray(w.reshape(k, nheads, 2, 32)[:, :, ::-1, :].reshape(k, nheads * 64))


def make_in_maps(inputs):
    f = lambda a: np.ascontiguousarray(np.asarray(a, dtype=np.float32))
    x = f(inputs["x"])
    c = f(inputs["c"])
    cb, onehot, pm, cosT, sinT = _consts()
    ada_bT = f(f(inputs["ada_b"]).reshape(4, 48, 128).transpose(2, 0, 1).reshape(128, 192))
    gainT = f(f(inputs["norm_gain"]).reshape(4, 2, 8, 128).transpose(3, 0, 1, 2).reshape(128, 64))
    mg = f(inputs["moba_qk_gain"])[0]
    sg = f(inputs["swa_qk_gain"])[0]
    sw = lambda g: np.concatenate([g[32:], g[:32]])
    cols = [mg[0], sw(mg[0]), mg[1], sw(mg[1]), sg[0], sw(sg[0]), sg[1], sw(sg[1])]
    qkgT = f(np.stack([np.tile(v, 2) for v in cols], axis=1))
    sinkT = f(np.tile(f(inputs["swa_sinks"])[0][None, :], (128, 1)))
    moba_w_in = f(inputs["moba_w_in"])
    swa_w_in = f(inputs["swa_w_in"])
    shared = {
        "ada_w": f(inputs["ada_w"]), "ada_bT": ada_bT, "gainT": gainT,
        "ffn_w_gate": f(inputs["ffn_w_gate"]), "ffn_w_up": f(inputs["ffn_w_up"]), "ffn_w_down": f(inputs["ffn_w_down"]),
        "sb_w_in": f(inputs["sb_w_in"]), "sb_w_out": f(inputs["sb_w_out"]),
        "moba_w_in": moba_w_in, "moba_w_sw": _swap_halves(moba_w_in[0][:, :2048], 32), "moba_w_out": f(inputs["moba_w_out"]),
        "swa_w_in": swa_w_in, "swa_w_sw": _swap_halves(swa_w_in[0][:, :1152], 18), "swa_w_out": f(inputs["swa_w_out"]),
        "qkgT": qkgT, "sinkT": sinkT, "cosT": cosT, "sinT": sinT, "constb": cb, "onehot": onehot, "pastmask": pm,
    }
    maps = []
    for b in range(8):
        m = dict(shared)
        m["xT"] = np.ascontiguousarray(x[b].T)
        m["cT"] = np.ascontiguousarray(c[b].reshape(8, 128).T)
        maps.append(m)
    return maps


_CACHE = {}


def kernel(**inputs):
    if "nc" not in _CACHE:
        _CACHE["nc"] = build_program()[0]
    nc = _CACHE["nc"]
    maps = make_in_maps(inputs)
    res = run_bass_kernel_spmd(nc, maps, core_ids=list(range(8)))
    out = np.stack([np.ascontiguousarray(res.results[b]["yT"].T) for b in range(8)], axis=0)
    return out.astype(np.float32)
```
